# Optimizing a Trainium2 kernel written in Bass

```python
import jax, jax.numpy as jnp
from jax import lax
import numpy as np

D_MODEL = 1024
BATCH = 8
SEQ = 8192
DEPTH = 1

CHUNK = 64
EPS = 1e-6
ADA_SLOTS = 9
N_NORMS = 6
D_FF = 2816
FFN_RES_W = 0.5
CONV_WIDTH = D_MODEL
CONV_K = 3
HG_HEADS = 8
HG_DK = D_MODEL // HG_HEADS
HG_DV = D_MODEL // HG_HEADS
HG_WIDTH = HG_HEADS * HG_DK
N_BRANCH = 2
MIX_WIDTHS = (CONV_WIDTH, CONV_WIDTH, CONV_WIDTH,
              HG_WIDTH, HG_WIDTH, HG_WIDTH, HG_WIDTH,
              D_MODEL, D_MODEL)
MIX_IN = sum(MIX_WIDTHS)
MIX_SPLITS = tuple(int(v) for v in np.cumsum(MIX_WIDTHS)[:-1])

kernel_name = "hybrid_conv_hgrn2_macaron_block"


def _rmsnorm(x, g):
    xf = x.astype(jnp.float32)
    xf = xf * lax.rsqrt(jnp.mean(xf * xf, axis=-1, keepdims=True) + EPS)
    return (xf * g.astype(jnp.float32)).astype(x.dtype)


def _modulate(h, shift, scale):
    return h * (1.0 + scale[:, None, :]) + shift[:, None, :]


def _swiglu(h, w_in, w_out):
    a, b = jnp.split(h @ w_in, 2, axis=-1)
    return (jax.nn.silu(a) * b) @ w_out


def _short_conv(u, w, b):
    s = u.shape[1]
    up = jnp.pad(u, ((0, 0), (CONV_K - 1, 0), (0, 0)))
    y = b[None, None, :]
    for j in range(CONV_K):
        y = y + w[j][None, None, :] * up[:, j:j + s, :]
    return y


def _to_chunks(t):
    b, s, h, d = t.shape
    return t.reshape(b, s // CHUNK, CHUNK, h, d).transpose(1, 0, 3, 2, 4)


def _hgrn2_chunkwise(q, k, v, log_f):
    bsz, s, h, _ = q.shape
    qc, kc, vc = _to_chunks(q), _to_chunks(k), _to_chunks(v)
    ac = jnp.cumsum(_to_chunks(log_f), axis=3)
    causal = jnp.tril(jnp.ones((CHUNK, CHUNK), dtype=bool))[:, :, None]

    def step(state, inp):
        q_i, k_i, v_i, a_i = inp
        o_inter = jnp.einsum('bhtk,bhkv->bhtv', q_i * jnp.exp(a_i), state)
        diff = a_i[:, :, :, None, :] - a_i[:, :, None, :, :]
        decay = jnp.exp(jnp.where(causal, diff, -jnp.inf))
        scores = jnp.einsum('bhtk,bhtsk,bhsk->bhts', q_i, decay, k_i)
        o_intra = jnp.einsum('bhts,bhsv->bhtv', scores, v_i)
        a_last = a_i[:, :, -1:, :]
        k_dec = k_i * jnp.exp(a_last - a_i)
        state = (jnp.exp(a_last[:, :, 0, :])[..., None] * state
                 + jnp.einsum('bhsk,bhsv->bhkv', k_dec, v_i))
        return state, o_inter + o_intra

    s0 = jnp.zeros((bsz, h, HG_DK, HG_DV), jnp.float32)
    _, oc = lax.scan(step, s0, (qc, kc, vc, ac))
    return oc.transpose(1, 0, 3, 2, 4).reshape(bsz, s, h, HG_DV)


def _mixer(h, w_mix_in, conv_w, conv_b, w_conv_out, hg_norm_g, lb, w_hg_out, w_mix_out):
    bsz, s, _ = h.shape
    u = h @ w_mix_in
    cb, cc, cv, hq, hf, hi, hg, ga, gb = jnp.split(u, MIX_SPLITS, axis=-1)

    ya = (cb * _short_conv(cc * cv, conv_w, conv_b)) @ w_conv_out

    hf32 = hf.astype(jnp.float32)
    lb32 = lb.astype(jnp.float32)
    log_f = jnp.logaddexp(jnp.log(lb32), jnp.log1p(-lb32) + jax.nn.log_sigmoid(hf32))
    k = -jnp.expm1(log_f)
    q = jax.nn.silu(hq.astype(jnp.float32))
    shp = (bsz, s, HG_HEADS, HG_DK)
    o = _hgrn2_chunkwise(q.reshape(shp), k.reshape(shp),
                         hi.astype(jnp.float32).reshape(bsz, s, HG_HEADS, HG_DV),
                         log_f.reshape(shp))
    o = _rmsnorm(o, hg_norm_g.reshape(HG_HEADS, HG_DV)).reshape(bsz, s, HG_WIDTH)
    o = (o * jax.nn.silu(hg.astype(jnp.float32))).astype(h.dtype)
    yb = o @ w_hg_out

    m = jax.nn.sigmoid(ga) * ya + jax.nn.sigmoid(gb) * yb
    return m @ w_mix_out


def setup_inputs(seed: int = 0) -> dict:
    key = jax.random.key(seed)
    ks = jax.random.split(key, 17)
    L, D = DEPTH, D_MODEL

    def nrm(k, shape, scale):
        return jax.random.normal(k, shape, jnp.float32) * scale

    return {
        "x": nrm(ks[0], (BATCH, SEQ, D), 1.0),
        "c": nrm(ks[1], (BATCH, D), 1.0),
        "w_ada": nrm(ks[2], (L, D, ADA_SLOTS * D), D ** -0.5),
        "b_ada": nrm(ks[3], (L, ADA_SLOTS * D), 0.02),
        "norm_gains": 1.0 + nrm(ks[4], (L, N_NORMS, D), 0.05),
        "w_ffn1_in": nrm(ks[5], (L, D, 2 * D_FF), D ** -0.5),
        "w_ffn1_out": nrm(ks[6], (L, D_FF, D), D_FF ** -0.5),
        "w_mix_in": nrm(ks[7], (L, D, MIX_IN), D ** -0.5),
        "conv_w": nrm(ks[8], (L, CONV_K, CONV_WIDTH), CONV_K ** -0.5),
        "conv_b": nrm(ks[9], (L, CONV_WIDTH), 0.02),
        "w_conv_out": nrm(ks[10], (L, CONV_WIDTH, D), CONV_WIDTH ** -0.5),
        "hg_norm_g": 1.0 + nrm(ks[11], (L, HG_WIDTH), 0.05),
        "lb_logits": nrm(ks[12], (L + 1, HG_WIDTH), 1.0),
        "w_hg_out": nrm(ks[13], (L, HG_WIDTH, D), HG_WIDTH ** -0.5),
        "w_mix_out": nrm(ks[14], (L, D, D), D ** -0.5),
        "w_ffn2_in": nrm(ks[15], (L, D, 2 * D_FF), D ** -0.5),
        "w_ffn2_out": nrm(ks[16], (L, D_FF, D), D_FF ** -0.5),
    }


def reference(x, c, w_ada, b_ada, norm_gains, w_ffn1_in, w_ffn1_out, w_mix_in, conv_w,
              conv_b, w_conv_out, hg_norm_g, lb_logits, w_hg_out, w_mix_out,
              w_ffn2_in, w_ffn2_out):
    lb_all = jnp.cumsum(jax.nn.softmax(lb_logits.astype(jnp.float32), axis=0), axis=0)
    lb_all = lb_all.astype(x.dtype)
    c_act = jax.nn.silu(c)
    for l in range(DEPTH):
        ada = c_act @ w_ada[l] + b_ada[l]
        sh1, sc1, g1, sh2, sc2, g2, sh3, sc3, g3 = jnp.split(ada, ADA_SLOTS, axis=-1)
        ng = norm_gains[l]

        h = _modulate(_rmsnorm(x, ng[0]), sh1, sc1)
        y = _rmsnorm(_swiglu(h, w_ffn1_in[l], w_ffn1_out[l]), ng[1])
        x = x + FFN_RES_W * g1[:, None, :] * y

        h = _modulate(_rmsnorm(x, ng[2]), sh2, sc2)
        y = _mixer(h, w_mix_in[l], conv_w[l], conv_b[l], w_conv_out[l], hg_norm_g[l],
                   lb_all[l], w_hg_out[l], w_mix_out[l])
        y = _rmsnorm(y, ng[3])
        x = x + g2[:, None, :] * y

        h = _modulate(_rmsnorm(x, ng[4]), sh3, sc3)
        y = _rmsnorm(_swiglu(h, w_ffn2_in[l], w_ffn2_out[l]), ng[5])
        x = x + FFN_RES_W * g3[:, None, :] * y
    return x
```

```python
from contextlib import ExitStack
import numpy as np
import concourse.bass as bass
import concourse.mybir as mybir
from concourse.bass_utils import run_bass_kernel_spmd

F32 = mybir.dt.float32
BF16 = mybir.dt.bfloat16
I32 = mybir.dt.int32
ALU = mybir.AluOpType
AF = mybir.ActivationFunctionType

SAME_ENGINE_SYNC = True

D = 1024
DFF = 2816
NJ = DFF // 128
TT = 512
NS = 3
EPS = 1e-6
NV = 184
NW = 66
GH = 2


class Buf:
    __slots__ = ("name", "w", "r", "rd", "dma_sem", "dma_cnt", "excl")

    def __init__(self, name, excl=False):
        self.name = name
        self.excl = excl
        self.w = None
        self.r = {}
        self.rd = []
        self.dma_sem = None
        self.dma_cnt = 0


class Prog:
    ENGS = ("pe", "act", "dve", "pool", "sp")

    def __init__(self, nc, stack):
        self.nc = nc
        self.stack = stack
        self.ops = {e: [] for e in self.ENGS}
        self.seen = {e: {} for e in self.ENGS}
        self.seen_dma = {e: {} for e in self.ENGS}
        self.sems = {e: stack.enter_context(nc.semaphore("s_" + e)) for e in self.ENGS}
        self.nsem = 0
        self.phase = ""

    def _need(self, eng, tok, waits):
        if tok is None:
            return
        if tok[0] == "e":
            _, src, idx = tok
            if src == eng and (eng == "pe" or not SAME_ENGINE_SYNC):
                return
            if self.seen[eng].get(src, -1) >= idx:
                return
            self.seen[eng][src] = idx
            self.ops[src][idx][2] = True
            waits.append(tok)
        else:
            _, buf, cnt = tok
            if self.seen_dma[eng].get(buf.name, 0) >= cnt:
                return
            self.seen_dma[eng][buf.name] = cnt
            waits.append(tok)

    def _deps(self, eng, reads, writes):
        waits = []
        for b in reads:
            self._need(eng, b.w, waits)
        for b in writes:
            self._need(eng, b.w, waits)
            for t in b.r.values():
                self._need(eng, t, waits)
            for t in b.rd:
                self._need(eng, t, waits)
        return waits

    def _commit(self, tok, reads, writes):
        if tok[0] == "e":
            for b in reads:
                b.r[tok[1]] = tok
        else:
            for b in reads:
                b.rd.append(tok)
        for b in writes:
            b.w = tok
            b.r = {}
            b.rd = []

    def op(self, eng, fn, reads=(), writes=()):
        if any(b.excl for b in reads):
            writes = list(writes) + [b for b in reads if b.excl]
            reads = [b for b in reads if not b.excl]
        waits = self._deps(eng, reads, writes)
        idx = len(self.ops[eng])
        self.ops[eng].append([fn, waits, False, None, self.phase])
        tok = ("e", eng, idx)
        self._commit(tok, reads, writes)
        return tok

    def dma(self, eng, fn, reads=(), writes=(), sem_buf=None):
        waits = self._deps(eng, reads, writes)
        if sem_buf.dma_sem is None:
            self.nsem += 1
            sem_buf.dma_sem = self.stack.enter_context(self.nc.semaphore("d_%d" % self.nsem))
        sem_buf.dma_cnt += 1
        tok = ("d", sem_buf, sem_buf.dma_cnt)
        self.ops[eng].append([fn, waits, False, sem_buf, self.phase])
        self._commit(tok, reads, writes)
        return tok

    def emit(self, final_toks):
        nc = self.nc
        final_waits = []
        for t in final_toks:
            self._need("sp", t, final_waits)
        val = {}
        for e in self.ENGS:
            c = 0
            for i, o in enumerate(self.ops[e]):
                if o[2]:
                    c += 1
                    val[(e, i)] = c
        sems = self.sems

        def do_wait(eng, t):
            if t[0] == "e":
                eng.wait_ge(sems[t[1]], val[(t[1], t[2])])
            else:
                eng.wait_ge(t[1].dma_sem, 16 * t[2])

        def run(ename, eng):
            for o in self.ops[ename]:
                for t in o[1]:
                    do_wait(eng, t)
                ins = o[0](eng)
                if o[3] is not None:
                    ins.then_inc(o[3].dma_sem, 16)
                elif o[2]:
                    ins.then_inc(sems[ename], 1)
            if ename == "sp":
                for t in final_waits:
                    do_wait(eng, t)

        with nc.Block() as block:
            @block.tensor
            def _(e):
                run("pe", e)

            @block.scalar
            def _(e):
                run("act", e)

            @block.vector
            def _(e):
                run("dve", e)

            @block.gpsimd
            def _(e):
                run("pool", e)

            @block.sync
            def _(e):
                run("sp", e)


def wtile_sizes():
    sizes = []
    for _ in range(2):
        pass
    order = []
    order += [("ffn1_in", g, 4096) for g in range(11)]
    order += [("ffn1_out", m, 2816) for m in range(8)]
    order += [("vproj", h, 4096) for h in range(2)]
    order += [("head", h, 3072) for h in range(8)]
    order += [("conv", c, 3072) for c in range(8)]
    order += [("merge", m, 4096) for m in range(8)]
    order += [("mixout", h, 4096) for h in range(2)]
    order += [("ffn2_in", g, 4096) for g in range(11)]
    order += [("ffn2_out", m, 2816) for m in range(8)]
    assert len(order) == NW
    return order


WORDER = wtile_sizes()


STOP = 9
LAST_PROG = None


def build_program(NT):
    nc = bass.Bass("TRN2", target_bir_lowering=False)
    SEQ = NT * TT
    x_d = nc.dram_tensor("x", [SEQ, D], F32, kind="ExternalInput").ap()
    vec_d = nc.dram_tensor("vec", [128, NV], F32, kind="ExternalInput").ap()
    wada_d = nc.dram_tensor("wada", [36, 128, 2048], F32, kind="ExternalInput").ap()
    wf_d = nc.dram_tensor("wf", [NW, 128, 4096], F32, kind="ExternalInput").ap()
    ident_d = nc.dram_tensor("ident", [128, 128], F32, kind="ExternalInput").ap()
    mask_d = nc.dram_tensor("masku", [64, 64], I32, kind="ExternalInput").ap()
    y_d = nc.dram_tensor("y", [SEQ, D], F32, kind="ExternalOutput").ap()
    wb_d = nc.dram_tensor("wb", [NW, 128, 4096], BF16, kind="Internal").ap()

    st = ExitStack()
    with st:
        P = Prog(nc, st)

        def sb(name, shape, dt):
            return st.enter_context(nc.sbuf_tensor("sb_" + name, shape, dt))

        ident_f = sb("ident_f", [128, 128], F32)
        ident_b = sb("ident_b", [128, 128], BF16)
        ones_b = sb("ones_b", [128, 128], BF16)
        mask_sb = sb("mask_sb", [64, 64], I32)
        negh = sb("negh", [128, 8], F32)
        epsv = sb("epsv", [128, 8], F32)
        rmask = sb("rmask", [128, TT], F32)
        vec = sb("vec", [128, NV], F32)
        cact = sb("cact", [128, 8], F32)
        ada = sb("ada", [128, 72], F32)
        gp = sb("gp", [128, 24], F32)
        gg = sb("gg", [128, 24], F32)
        lbv = sb("lbv", [128, 40], F32)
        S = sb("S", [128, 8, 128], F32)
        Sbf = sb("Sbf", [128, 8, 128], BF16)
        Sdl = sb("Sdl", [128, 8, 128], F32)
        hist = sb("hist", [128, 8, 2], F32)
        dmid = sb("dmid", [128, 8, 8], F32)
        dl = sb("dl", [128, 8, 8], F32)
        dlm = sb("dlm", [128, 8, 8], F32)
        dtmp = sb("dtmp", [128, 8, 8], F32)
        scs = [sb("scs%d" % i, [64, 64], BF16) for i in range(4)]
        xin_t = [sb("xin%d" % i, [128, D], F32) for i in range(2)]
        xout_t = [sb("xout%d" % i, [128, D], F32) for i in range(2)]
        xT = sb("xT", [128, 8, TT], F32)
        xT1 = sb("xT1", [128, 8, TT], F32)
        rstd2 = sb("rstd2", [128, TT], F32)
        hT = sb("hT", [128, 8, TT], BF16)
        yT = sb("yT", [128, 8, TT], F32)
        sq = [sb("sq%d" % i, [128, TT], BF16) for i in range(3)]
        TR = [sb("tr%d" % i, [128, TT], F32) for i in range(3)]
        ms = sb("ms", [128, TT], F32)
        rstd = sb("rstd", [128, TT], F32)
        AR = sb("AR", [128, 24, TT], BF16)
        vT = sb("vT", [128, 8, TT], BF16)
        gsil = sb("gsil", [128, 8, TT], BF16)
        ogT = sb("ogT", [128, 8, TT], BF16)
        mT = sb("mT", [128, 8, TT], BF16)
        ccvw = [sb("ccvw%d" % i, [128, TT + 2], F32) for i in range(2)]
        ktok = sb("ktok", [64, GH, 8, 128], BF16)
        vtok = sb("vtok", [64, GH, 8, 128], BF16)
        wslot = [sb("wslot%d" % i, [128, 4096], BF16) for i in range(NS)]
        wstg = [sb("wstg%d" % i, [128, 1024], F32) for i in range(3)]

        banks = [st.enter_context(nc.psum_tensor("bank%d" % i, [128, 512], F32)) for i in range(8)]

        def bl(prefix, n, excl=False):
            return [Buf("%s%d" % (prefix, i), excl) for i in range(n)]

        bCONST = Buf("const")
        bVEC = Buf("vec")
        bXT, bHT, bYT = bl("xT", 8), bl("hT", 8), bl("yT", 8)
        bAR = bl("AR", 24)
        bVT, bGS, bOG, bMT = bl("vT", 8), bl("gs", 8), bl("og", 8), bl("mT", 8)
        bSQ, bTR = bl("sq", 3), bl("tr", 3)
        bMS, bRSTD = Buf("ms"), Buf("rstd")
        bRSTD2 = Buf("rstd2")
        bXT1 = bl("xU", 8)
        xTb, bXTb = [xT, xT1], [bXT, bXT1]
        cur = {"X": xT, "bX": bXT}
        bBANK = bl("bank", 8, True)
        bSS = bBANK[4]
        RING = [0, 1, 2, 3, 5, 6, 7]
        bSCS = bl("scs", 4)
        bS, bSbf, bSdl = bl("S", 8), bl("Sbf", 8), bl("Sdl", 8)
        bHIST = bl("hist", 8)
        bDEC = bl("dec", 8)
        bDTMP = Buf("dtmp")
        bKTOK, bVTOK = bl("ktok", GH), bl("vtok", GH)
        bXIN, bXOUT = bl("xin", 2), bl("xout", 2)
        bWS = bl("ws", NS)
        bWSTG = bl("wstg", 3)
        bCCVW = bl("ccvw", 2)
        bWB = [Buf("wb%d" % w) for w in range(NW)]
        bADA = bSS

        rr = {"wstg": 0, "bank": 0, "tr": 0, "sq": 0, "sc": 0, "kv": 0, "scs": 0, "ccvw": 0, "xin": 0, "xout": 0}

        def nxt(kind, n):
            i = rr[kind]
            rr[kind] = (i + 1) % n
            return i

        def bank():
            i = RING[nxt("bank", 7)]
            return banks[i], bBANK[i]

        def tr():
            i = nxt("tr", 3)
            return TR[i], bTR[i]

        def act_(out, in_, func, R, W, **kw):
            return P.op("act", lambda e: e.activation(out=out, in_=in_, func=func, **kw), reads=R, writes=W)

        def tt_(eng, out, a, b, op, R, W):
            return P.op(eng, lambda e: e.tensor_tensor(out=out, in0=a, in1=b, op=op), reads=R, writes=W)

        def ts_(eng, out, a, s1, s2, op0, op1, R, W):
            return P.op(eng, lambda e: e.tensor_scalar(out=out, in0=a, scalar1=s1, scalar2=s2, op0=op0, op1=op1),
                        reads=R, writes=W)

        def stt_(out, a, s, b, op0, op1, R, W):
            return P.op("dve", lambda e: e.scalar_tensor_tensor(out=out, in0=a, scalar=s, in1=b, op0=op0, op1=op1),
                        reads=R, writes=W)

        def copy_(eng, out, in_, R, W):
            if eng == "act":
                return act_(out, in_, AF.Copy, R, W)
            return P.op(eng, lambda e: e.tensor_copy(out=out, in_=in_), reads=R, writes=W)

        def mm_(out, lhsT, rhs, start, stop, R, W):
            return P.op("pe", lambda e: e.matmul(out, lhsT=lhsT, rhs=rhs, start=start, stop=stop), reads=R, writes=W)

        def tp_(out, in_, ident, R, W):
            return P.op("pe", lambda e: e.transpose(out=out, in_=in_, identity=ident), reads=R, writes=W)

        def memset_(eng, ap, v, W):
            return P.op(eng, lambda e: e.memset(ap, v), writes=W)

        P.dma("sp", lambda e: e.dma_start(out=vec[:], in_=vec_d[:, :]), writes=[bVEC], sem_buf=bVEC)
        bID = Buf("identf")
        bMK = Buf("mask")
        P.dma("sp", lambda e: e.dma_start(out=ident_f[:], in_=ident_d[:, :]), writes=[bID], sem_buf=bID)
        P.dma("sp", lambda e: e.dma_start(out=mask_sb[:], in_=mask_d[:, :]), writes=[bMK], sem_buf=bMK)
        copy_("dve", ident_b[:], ident_f[:], [bID], [bCONST])
        memset_("pool", ones_b[:], 1.0, [bCONST])
        memset_("pool", negh[:], -0.5, [bCONST])
        memset_("pool", epsv[:], EPS, [bCONST])
        memset_("pool", rmask[:], 1.0, [bCONST])
        memset_("pool", rmask[:].rearrange("p (c t) -> p c t", t=64)[:, :, 0:1], 0.0, [bCONST])
        memset_("pool", S[:], 0.0, bS)
        memset_("pool", hist[:], 0.0, bHIST)
        for i in range(4):
            memset_("pool", scs[i][:], 0.0, [bSCS[i]])

        V_C, V_BADA, V_NG, V_CW, V_CB, V_HGN, V_LB = 0, 8, 80, 128, 152, 160, 168
        act_(cact[:], vec[:, V_C:V_C + 8], AF.Silu, [bVEC], [bCONST])
        for i in range(36):
            k = i % 2
            stg2 = yT[:, 4 * k:4 * k + 4, :].rearrange("p a b -> p (a b)")
            stg = stg2.rearrange("p (c n) -> p c n", c=8)
            sbufs = bYT[4 * k:4 * k + 4]
            P.dma("sp", lambda e, stg2=stg2, i=i: e.dma_start(out=stg2, in_=wada_d[i, :, :]),
                  writes=sbufs, sem_buf=sbufs[0])
            for nb in range(2):
                blk = i * 2 + nb
                for c in range(8):
                    mm_(banks[4][:, blk:blk + 1], stg[:, c, nb * 128:(nb + 1) * 128], cact[:, c:c + 1],
                        c == 0, c == 7, sbufs + [bCONST], [bADA])
        tt_("dve", ada[:], banks[4][:, 0:72], vec[:, V_BADA:V_BADA + 72], ALU.add, [bADA, bVEC], [bCONST])
        RW = [0.5, 1.0, 0.5]
        for i in range(3):
            sh_c, sc_c, g_c = (3 * i) * 8, (3 * i + 1) * 8, (3 * i + 2) * 8
            stt_(gp[:, i * 8:(i + 1) * 8], ada[:, sc_c:sc_c + 8], 1.0,
                 vec[:, V_NG + (2 * i) * 8:V_NG + (2 * i) * 8 + 8], ALU.add, ALU.mult, [bCONST, bVEC], [bCONST])
            stt_(gg[:, i * 8:(i + 1) * 8], ada[:, g_c:g_c + 8], RW[i],
                 vec[:, V_NG + (2 * i + 1) * 8:V_NG + (2 * i + 1) * 8 + 8], ALU.mult, ALU.mult, [bCONST, bVEC], [bCONST])

        def shv(i, c):
            return ada[:, (3 * i) * 8 + c:(3 * i) * 8 + c + 1]

        tt_("dve", lbv[:, 32:40], vec[:, V_LB:V_LB + 8], vec[:, V_LB + 8:V_LB + 16], ALU.subtract, [bVEC], [bCONST])
        act_(lbv[:, 0:8], lbv[:, 32:40], AF.Sigmoid, [bCONST], [bCONST])
        ts_("dve", lbv[:, 8:16], lbv[:, 0:8], 0.5, 0.5, ALU.mult, ALU.add, [bCONST], [bCONST])
        ts_("dve", lbv[:, 16:24], lbv[:, 0:8], -0.5, 0.5, ALU.mult, ALU.add, [bCONST], [bCONST])
        ts_("dve", lbv[:, 24:32], lbv[:, 0:8], 0.5, -0.5, ALU.mult, ALU.add, [bCONST], [bCONST])

        def _en(kind):
            if kind.startswith("ffn1"):
                return STOP >= 4
            if kind.startswith("ffn2"):
                return STOP >= 6
            return STOP >= 5
        wseq = [w for _ in range(NT) for w in range(NW) if _en(WORDER[w][0])]
        ws = {"issued": 0, "used": 0}

        n_first = len(wseq) // NT

        def w_issue():
            n = ws["issued"]
            w = wseq[n]
            E = WORDER[w][2]
            s = n % NS
            if n < n_first:
                for off in range(0, E, 1024):
                    m = min(1024, E - off)
                    k = nxt("wstg", 3)
                    P.dma("sp", lambda e, k=k, off=off, m=m: e.dma_start(out=wstg[k][:, 0:m], in_=wf_d[w, :, off:off + m]),
                          writes=[bWSTG[k]], sem_buf=bWSTG[k])
                    ceng = ("pool", "dve", "pool", "act")[(off // 1024) % 4]
                    copy_(ceng, wslot[s][:, off:off + m], wstg[k][:, 0:m], [bWSTG[k]], [bWS[s]])
                P.dma("pool", lambda e: e.dma_start(out=wb_d[w, :, 0:E], in_=wslot[s][:, 0:E]),
                      reads=[bWS[s]], writes=[bWB[w]], sem_buf=bWB[w])
            else:
                P.dma("sp", lambda e: e.dma_start(out=wslot[s][:, 0:E], in_=wb_d[w, :, 0:E]),
                      reads=[bWB[w]], writes=[bWS[s]], sem_buf=bWS[s])
            ws["issued"] = n + 1

        def w_next(kind):
            n = ws["used"]
            assert WORDER[wseq[n]][0] == kind, (WORDER[wseq[n]], kind)
            while ws["issued"] < min(len(wseq), n + NS):
                w_issue()
            ws["used"] = n + 1
            s = n % NS
            return wslot[s], bWS[s]

        def rms_stats_finish(inv_n, ss_ap=None, bss=None, rs=None, brs=None):
            if ss_ap is None:
                ss_ap, bss, rs, brs = banks[4][:, :], bSS, rstd, bRSTD
                m_, bm_ = ms, bMS
            else:
                m_, bm_ = tr()
            act_(m_[:], ss_ap, AF.Ln, [bss, bCONST], [bm_], scale=inv_n, bias=epsv[:, 0:1])
            act_(rs[:], m_[:], AF.Exp, [bm_], [brs], scale=-0.5)

        def prenorm(i, alt=False):
            P.phase = "prenorm%d" % i
            X, bX = cur["X"], cur["bX"]
            if alt:
                ss_t, bss = bank()
                ss_ap, rs, brs = ss_t[:, :], rstd2, bRSTD2
            else:
                ss_ap, bss, rs, brs = banks[4][:, :], bSS, rstd, bRSTD
            for c in range(8):
                r = nxt("sq", 3)
                act_(sq[r][:], X[:, c, :], AF.Square, [bX[c]], [bSQ[r]])
                mm_(ss_ap, ones_b[:], sq[r][:], c == 0, c == 7, [bCONST, bSQ[r]], [bss])
            rms_stats_finish(1.0 / D, ss_ap, bss, rs, brs)
            for c in range(8):
                t, bt_ = tr()
                tt_("dve", t[:], X[:, c, :], rs[:], ALU.mult, [bX[c], brs], [bt_])
                act_(hT[:, c, :], t[:], AF.Identity, [bt_, bCONST], [bHT[c]],
                     scale=gp[:, i * 8 + c:i * 8 + c + 1], bias=shv(i, c))

        def y_block_done(m, y_ps, by):
            copy_("dve", yT[:, m, :], y_ps[:, :], [by], [bYT[m]])
            r = nxt("sq", 3)
            act_(sq[r][:], yT[:, m, :], AF.Square, [bYT[m]], [bSQ[r]])
            return r

        def postnorm(i):
            P.phase = "postnorm%d" % i
            X, bX = cur["X"], cur["bX"]
            rms_stats_finish(1.0 / D)
            for c in range(8):
                t, bt_ = tr()
                tt_("dve", t[:], yT[:, c, :], rstd[:], ALU.mult, [bYT[c], bRSTD], [bt_])
                stt_(X[:, c, :], t[:], gg[:, i * 8 + c:i * 8 + c + 1], X[:, c, :], ALU.mult, ALU.add,
                     [bt_, bCONST, bX[c]], [bX[c]])

        def out_blocks(kind, nblk_per_tile, ntiles, rhs_of, rbufs_of, nk, hook=None):
            pend = None
            for wt_i in range(ntiles):
                wt, bw = w_next(kind)
                wv = wt[:, 0:nk * nblk_per_tile * 128].rearrange("p (k n) -> p k n", k=nk)
                for mm in range(nblk_per_tile):
                    m = wt_i * nblk_per_tile + mm
                    y_ps, by = bank()
                    for k in range(nk):
                        mm_(y_ps[:, :], wv[:, k, mm * 128:(mm + 1) * 128], rhs_of(k), k == 0, k == nk - 1,
                            [bw] + rbufs_of(k), [by])
                    if pend is not None:
                        mm_(banks[4][:, :], ones_b[:], sq[pend[0]][:], pend[1] == 0, False,
                            [bCONST, bSQ[pend[0]]], [bSS])
                    r = y_block_done(m, y_ps, by)
                    pend = (r, m)
                    if hook is not None and m == 1:
                        mm_(banks[4][:, :], ones_b[:], sq[pend[0]][:], pend[1] == 0, False,
                            [bCONST, bSQ[pend[0]]], [bSS])
                        pend = None
                        hook()
            mm_(banks[4][:, :], ones_b[:], sq[pend[0]][:], False, True, [bCONST, bSQ[pend[0]]], [bSS])

        def ffn(i, kin, kout, pre=True, mid_hook=None, out_hook=None):
            if pre:
                prenorm(i)
            for g in range(11):
                P.phase = "ffn%d_in" % i
                if mid_hook is not None and g == 2:
                    mid_hook()
                    P.phase = "ffn%d_in" % i
                wt, bw = w_next(kin)
                wv = wt[:, :].rearrange("p (c n) -> p c n", c=8)
                for jj in range(2):
                    j = 2 * g + jj
                    a_ps, ba_ = bank()
                    b_ps, bb_ = bank()
                    for c in range(8):
                        mm_(a_ps[:, :], wv[:, c, jj * 128:(jj + 1) * 128], hT[:, c, :], c == 0, c == 7,
                            [bw, bHT[c]], [ba_])
                    for c in range(8):
                        mm_(b_ps[:, :], wv[:, c, 256 + jj * 128:256 + (jj + 1) * 128], hT[:, c, :], c == 0, c == 7,
                            [bw, bHT[c]], [bb_])
                    t, bt_ = tr()
                    act_(t[:], a_ps[:, :], AF.Silu, [ba_], [bt_])
                    tt_("dve", AR[:, j, :], t[:], b_ps[:, :], ALU.mult, [bt_, bb_], [bAR[j]])
            P.phase = "ffn%d_out" % i
            def oh():
                out_hook()
                P.phase = "ffn%d_out" % i
            out_blocks(kout, 1, 8, lambda k: AR[:, k, :], lambda k: [bAR[k]], NJ, hook=(oh if out_hook else None))
            postnorm(i)

        def mixer():
            i = 1
            prenorm(i)
            P.phase = "vproj"
            for hg_ in range(2):
                wt, bw = w_next("vproj")
                wv = wt[:, :].rearrange("p (c n) -> p c n", c=8)
                for hh in range(4):
                    hd = hg_ * 4 + hh
                    v_ps, bv_ = bank()
                    for k in range(8):
                        mm_(v_ps[:, :], wv[:, k, hh * 128:(hh + 1) * 128], hT[:, k, :], k == 0, k == 7,
                            [bw, bHT[k]], [bv_])
                    copy_("act" if hh % 2 == 0 else "dve", vT[:, hd, :], v_ps[:, :], [bv_], [bVT[hd]])
            qT = lambda hd: AR[:, 8 + hd, :]
            kT = lambda hd: AR[:, 16 + hd, :]
            P.phase = "heads"

            def TY(i):
                return yT[:, i, :], bYT[i]

            for pr in range(4):
                hds = [2 * pr, 2 * pr + 1]
                proj = {}
                for hi_, hd in enumerate(hds):
                    wt, bw = w_next("head")
                    wv = wt[:, 0:3072].rearrange("p (c n) -> p c n", c=8)
                    q_ps, bqp = bank()
                    f_ps, bfp = bank()
                    g_ps, bgp = bank()
                    for (ps_, bp_, o_) in ((q_ps, bqp, 0), (f_ps, bfp, 128), (g_ps, bgp, 256)):
                        for k in range(8):
                            mm_(ps_[:, :], wv[:, k, o_:o_ + 128], hT[:, k, :], k == 0, k == 7, [bw, bHT[k]], [bp_])
                    proj[hd] = (q_ps, bqp, f_ps, bfp, g_ps, bgp)
                for hi_, hd in enumerate(hds):
                    (q_ps, bqp, f_ps, bfp, g_ps, bgp) = proj[hd]
                    T1, b1 = TY(4 * hi_)
                    act_(qT(hd), q_ps[:, :], AF.Silu, [bqp], [bAR[8 + hd]])
                    act_(T1, f_ps[:, :], AF.Tanh, [bfp], [b1], scale=0.5)
                    act_(gsil[:, hd, :], g_ps[:, :], AF.Silu, [bgp], [bGS[hd]])
                for hi_, hd in enumerate(hds):
                    T1, b1 = TY(4 * hi_)
                    T2, b2 = TY(4 * hi_ + 1)
                    c0 = lbv[:, 8 + hd:9 + hd]
                    c1 = lbv[:, 16 + hd:17 + hd]
                    nc1 = lbv[:, 24 + hd:25 + hd]
                    ts_("pool", T2, T1, nc1, c1, ALU.mult, ALU.add, [b1, bCONST], [b2])
                    ts_("dve", T1, T1, c1, c0, ALU.mult, ALU.add, [b1, bCONST], [b1])
                for hi_, hd in enumerate(hds):
                    T1, b1 = TY(4 * hi_)
                    act_(T1, T1, AF.Ln, [b1], [b1])
                for hi_, hd in enumerate(hds):
                    T1, b1 = TY(4 * hi_)
                    T3, b3 = TY(4 * hi_ + 2)
                    P.op("dve", lambda e, T1=T1, T3=T3: e.tensor_tensor_scan(out=T3, data0=rmask[:], data1=T1, initial=0.0,
                                                                            op0=ALU.mult, op1=ALU.add),
                         reads=[bCONST, b1], writes=[b3])
                    a3 = T3.rearrange("p (c t) -> p c t", t=64)
                    tt_("dve", dtmp[:, hd, :], a3[:, :, 63], a3[:, :, 31], ALU.subtract, [b3], [bDTMP])
                for hi_, hd in enumerate(hds):
                    T3, b3 = TY(4 * hi_ + 2)
                    a3 = T3.rearrange("p (c t) -> p c t", t=64)
                    act_(dmid[:, hd, :], a3[:, :, 31], AF.Exp, [b3], [bDEC[hd]])
                    act_(dl[:, hd, :], a3[:, :, 63], AF.Exp, [b3], [bDEC[hd]])
                    act_(dlm[:, hd, :], dtmp[:, hd, :], AF.Exp, [bDTMP], [bDEC[hd]])
                for hi_, hd in enumerate(hds):
                    T1, b1 = TY(4 * hi_)
                    T2, b2 = TY(4 * hi_ + 1)
                    T3, b3 = TY(4 * hi_ + 2)
                    T4, b4 = TY(4 * hi_ + 3)
                    a3 = T3.rearrange("p (c t) -> p c t", t=64)
                    E3 = T4.rearrange("p (c t) -> p c t", t=64)
                    tt_("dve", E3, a3[:, :, 31:32].to_broadcast([128, 8, 64]), a3, ALU.subtract, [b3], [b4])
                    act_(T1, T4, AF.Exp, [b4], [b1])
                    act_(T4, T4, AF.Exp, [b4], [b4], scale=-1.0)
                    tt_("dve", kT(hd), T2, T1, ALU.mult, [b2, b1], [bAR[16 + hd]])
                    tt_("pool", qT(hd), qT(hd), T4, ALU.mult, [bAR[8 + hd], b4], [bAR[8 + hd]])

            P.phase = "conv"
            for c in range(8):
                wt, bw = w_next("conv")
                wv = wt[:, 0:3072].rearrange("p (c n) -> p c n", c=8)
                cc_ps, bcc = bank()
                cv_ps, bcv = bank()
                cb_ps, bcb = bank()
                for (ps_, bp_, o_) in ((cc_ps, bcc, 0), (cv_ps, bcv, 128), (cb_ps, bcb, 256)):
                    for k in range(8):
                        mm_(ps_[:, :], wv[:, k, o_:o_ + 128], hT[:, k, :], k == 0, k == 7, [bw, bHT[k]], [bp_])
                t, bt_ = tr()
                copy_("act", t[:], cv_ps[:, :], [bcv], [bt_])
                r = nxt("ccvw", 2)
                cw = ccvw[r]
                bcw = bCCVW[r]
                copy_("pool", cw[:, 0:2], hist[:, c, :], [bHIST[c]], [bcw])
                tt_("dve", cw[:, 2:TT + 2], cc_ps[:, :], t[:], ALU.mult, [bcc, bt_, bcw], [bcw])
                copy_("pool", hist[:, c, :], cw[:, TT:TT + 2], [bcw], [bHIST[c]])
                a_, ba2 = tr()
                w0 = vec[:, V_CW + 0 * 8 + c:V_CW + 0 * 8 + c + 1]
                w1 = vec[:, V_CW + 1 * 8 + c:V_CW + 1 * 8 + c + 1]
                w2 = vec[:, V_CW + 2 * 8 + c:V_CW + 2 * 8 + c + 1]
                cbv = vec[:, V_CB + c:V_CB + c + 1]
                ts_("dve", a_[:], cw[:, 2:TT + 2], w2, cbv, ALU.mult, ALU.add, [bcw, bVEC], [ba2])
                stt_(a_[:], cw[:, 1:TT + 1], w1, a_[:], ALU.mult, ALU.add, [bcw, bVEC, ba2], [ba2])
                stt_(a_[:], cw[:, 0:TT], w0, a_[:], ALU.mult, ALU.add, [bcw, bVEC, ba2], [ba2])
                tt_("dve", AR[:, c, :], a_[:], cb_ps[:, :], ALU.mult, [ba2, bcb], [bAR[c]])
            for gi in range(8 // GH):
                heads = list(range(gi * GH, (gi + 1) * GH))
                P.phase = "rec_tp"
                for hi_, hd in enumerate(heads):
                    for (src, bsrc, dst, bdst) in ((kT(hd), bAR[16 + hd], ktok, bKTOK[hi_]),
                                                   (vT[:, hd, :], bVT[hd], vtok, bVTOK[hi_])):
                        tp_ps, btp = bank()
                        psb = tp_ps[:, :].bitcast(BF16)
                        for ch in range(8):
                            tp_(psb[0:64, ch * 128:(ch + 1) * 128], src[:, ch * 64:(ch + 1) * 64], ident_b[:],
                                [bsrc, bCONST], [btp])
                        copy_("dve" if hi_ % 2 == 0 else "act",
                              dst[:, hi_, :, :].rearrange("p a b -> p (a b)"), psb[0:64, 0:1024], [btp], [bdst])
                obanks = [bank() for _ in heads]
                scb = [bank(), bank()]
                kvb = bank()
                P.phase = "rec"
                steps = [(ch, hi_, hd) for ch in range(8) for hi_, hd in enumerate(heads)]

                def issue_sc(n):
                    ch, hi_, hd = steps[n]
                    cs = slice(ch * 64, (ch + 1) * 64)
                    sc_t, bsc = scb[n % 2]
                    sc_ps = sc_t[0:64, 0:64]
                    mm_(sc_ps, kT(hd)[:, cs], qT(hd)[:, cs], True, True, [bAR[16 + hd], bAR[8 + hd]], [bsc])
                    ri = nxt("scs", 4)
                    P.op("dve", lambda e, ri=ri, sc_ps=sc_ps: e.copy_predicated(out=scs[ri][:], mask=mask_sb[:], data=sc_ps),
                         reads=[bsc, bMK], writes=[bSCS[ri]])
                    return ri

                ri_next = issue_sc(0)
                for n, (ch, hi_, hd) in enumerate(steps):
                    o_ps, bo = obanks[hi_]
                    cs = slice(ch * 64, (ch + 1) * 64)
                    ri = ri_next
                    act_(Sbf[:, hd, :], S[:, hd, :], AF.Identity, [bS[hd], bDEC[hd]], [bSbf[hd]],
                         scale=dmid[:, hd, ch:ch + 1])
                    ts_("dve", Sdl[:, hd, :], S[:, hd, :], dl[:, hd, ch:ch + 1], None, ALU.mult, ALU.bypass,
                        [bS[hd], bDEC[hd]], [bSdl[hd]])
                    if n + 1 < len(steps):
                        ri_next = issue_sc(n + 1)
                    mm_(o_ps[:, cs], vtok[:, hi_, ch, :], scs[ri][:], True, False, [bVTOK[hi_], bSCS[ri]], [bo])
                    mm_(o_ps[:, cs], Sbf[:, hd, :], qT(hd)[:, cs], False, True, [bSbf[hd], bAR[8 + hd]], [bo])
                    kv_ps = kvb[0][:, 0:128]
                    mm_(kv_ps, ktok[:, hi_, ch, :], vtok[:, hi_, ch, :], True, True, [bKTOK[hi_], bVTOK[hi_]], [kvb[1]])
                    stt_(S[:, hd, :], kv_ps, dlm[:, hd, ch:ch + 1], Sdl[:, hd, :], ALU.mult, ALU.add,
                         [kvb[1], bDEC[hd], bSdl[hd]], [bS[hd]])
                P.phase = "onorm"
                for hi_, hd in enumerate(heads):
                    o_ps, bo = obanks[hi_]
                    r = nxt("sq", 3)
                    act_(sq[r][:], o_ps[:, :], AF.Square, [bo], [bSQ[r]])
                    mm_(banks[4][:, :], ones_b[:], sq[r][:], True, True, [bCONST, bSQ[r]], [bSS])
                    rms_stats_finish(1.0 / 128)
                    t, bt_ = tr()
                    tt_("dve", t[:], o_ps[:, :], rstd[:], ALU.mult, [bo, bRSTD], [bt_])
                    stt_(ogT[:, hd, :], t[:], vec[:, V_HGN + hd:V_HGN + hd + 1], gsil[:, hd, :], ALU.mult, ALU.mult,
                         [bt_, bVEC, bGS[hd]], [bOG[hd]])
            P.phase = "merge"
            for m in range(8):
                wt, bw = w_next("merge")
                wv = wt[:, :].rearrange("p (c n) -> p c n", c=8)
                ya_ps, bya = bank()
                yb_ps, byb = bank()
                ga_ps, bga = bank()
                gb_ps, bgb = bank()
                for k in range(8):
                    mm_(ya_ps[:, :], wv[:, k, 0:128], AR[:, k, :], k == 0, k == 7, [bw, bAR[k]], [bya])
                for k in range(8):
                    mm_(yb_ps[:, :], wv[:, k, 128:256], ogT[:, k, :], k == 0, k == 7, [bw, bOG[k]], [byb])
                for k in range(8):
                    mm_(ga_ps[:, :], wv[:, k, 256:384], hT[:, k, :], k == 0, k == 7, [bw, bHT[k]], [bga])
                for k in range(8):
                    mm_(gb_ps[:, :], wv[:, k, 384:512], hT[:, k, :], k == 0, k == 7, [bw, bHT[k]], [bgb])
                o4 = 4 * (m % 2)
                Y0, y0 = yT[:, o4, :], bYT[o4]
                Y1, y1 = yT[:, o4 + 1, :], bYT[o4 + 1]
                Y2, y2 = yT[:, o4 + 2, :], bYT[o4 + 2]
                Y3, y3 = yT[:, o4 + 3, :], bYT[o4 + 3]
                act_(Y0, ga_ps[:, :], AF.Sigmoid, [bga], [y0])
                act_(Y1, gb_ps[:, :], AF.Sigmoid, [bgb], [y1])
                tt_("dve", Y2, Y0, ya_ps[:, :], ALU.mult, [y0, bya], [y2])
                tt_("dve", Y3, Y1, yb_ps[:, :], ALU.mult, [y1, byb], [y3])
                tt_("pool", mT[:, m, :], Y2, Y3, ALU.add, [y2, y3], [bMT[m]])
            P.phase = "mixout"
            out_blocks("mixout", 4, 2, lambda k: mT[:, k, :], lambda k: [bMT[k]], 8)
            postnorm(i)

        def load_x(it, s):
            r = nxt("xin", 2)
            row = it * TT + s * 128
            P.dma("sp", lambda e: e.dma_start(out=xin_t[r][:], in_=x_d[row:row + 128, :]),
                  writes=[bXIN[r]], sem_buf=bXIN[r])
            return r

        final = []
        xq = []

        def xin(it):
            P.phase = "xin"
            X, bX = cur["X"], cur["bX"]
            if it == 0:
                xq.append(load_x(0, 0))
                xq.append(load_x(0, 1))
            for s in range(4):
                r = xq.pop(0)
                for half in range(2):
                    tp_ps, btp = bank()
                    for cc in range(4):
                        c = half * 4 + cc
                        tp_(tp_ps[:, cc * 128:(cc + 1) * 128], xin_t[r][:, c * 128:(c + 1) * 128], ident_f[:],
                            [bXIN[r], bID], [btp])
                    copy_("act" if half == 0 else "dve",
                          X[:, half * 4:half * 4 + 4, s * 128:(s + 1) * 128],
                          tp_ps[:, :].rearrange("p (c t) -> p c t", c=4), [btp], bX[half * 4:half * 4 + 4])
                if s < 2:
                    xq.append(load_x(it, s + 2))
                elif it + 1 < NT:
                    xq.append(load_x(it + 1, s - 2))

        def xout(it, X, bX):
            P.phase = "xout"
            for s in range(4):
                r = nxt("xout", 2)
                for half in range(2):
                    tp_ps, btp = bank()
                    for cc in range(4):
                        c = half * 4 + cc
                        tp_(tp_ps[:, cc * 128:(cc + 1) * 128], X[:, c, s * 128:(s + 1) * 128], ident_f[:],
                            [bX[c], bID], [btp])
                    copy_("act" if half == 0 else "dve", xout_t[r][:, half * 512:(half + 1) * 512], tp_ps[:, :],
                          [btp], [bXOUT[r]])
                row = it * TT + s * 128
                final.append(P.dma("act", lambda e, r=r, row=row: e.dma_start(out=y_d[row:row + 128, :], in_=xout_t[r][:]),
                                   reads=[bXOUT[r]], sem_buf=bXOUT[r]))

        def prep(it, alt):
            cur["X"], cur["bX"] = xTb[it % 2], bXTb[it % 2]
            xin(it)
            prenorm(0, alt=alt)

        prep(0, False)
        pend_xout = None
        for it in range(NT):
            cur["X"], cur["bX"] = xTb[it % 2], bXTb[it % 2]

            def out_hook(it=it):
                if it + 1 < NT:
                    prep(it + 1, True)
                    cur["X"], cur["bX"] = xTb[it % 2], bXTb[it % 2]

            hooked = False
            if STOP >= 4:
                ffn(0, "ffn1_in", "ffn1_out", pre=False, mid_hook=pend_xout)
                pend_xout = None
            if pend_xout is not None:
                pend_xout()
                pend_xout = None
            if STOP >= 5:
                mixer()
            if STOP >= 6:
                ffn(2, "ffn2_in", "ffn2_out", out_hook=out_hook)
            else:
                out_hook()
            pend_xout = (lambda it=it, X=cur["X"], bX=cur["bX"]: xout(it, X, bX))
        pend_xout()
        P.emit(final)
        global LAST_PROG
        LAST_PROG = P
    return nc


def _pc(v):
    return np.ascontiguousarray(v.reshape(-1, 128).T)


def _in_tile(W, cols):
    sub = W[:, cols]
    n = sub.shape[1]
    return np.ascontiguousarray(sub.reshape(8, 128, n).transpose(1, 0, 2)).reshape(128, 8 * n)


def _layout_weights(w_ffn1_in, w_ffn1_out, w_mix_in, w_conv_out, w_hg_out, w_mix_out, w_ffn2_in, w_ffn2_out):
    wf = np.zeros((NW, 128, 4096), np.float32)
    r128 = np.arange(128)
    for w, (kind, idx, E) in enumerate(WORDER):
        if kind in ("ffn1_in", "ffn2_in"):
            W = w_ffn1_in if kind == "ffn1_in" else w_ffn2_in
            g = idx
            cols = np.concatenate([(2 * g) * 128 + r128, (2 * g + 1) * 128 + r128,
                                   DFF + (2 * g) * 128 + r128, DFF + (2 * g + 1) * 128 + r128])
            t = _in_tile(W, cols)
        elif kind in ("ffn1_out", "ffn2_out"):
            W = w_ffn1_out if kind == "ffn1_out" else w_ffn2_out
            m = idx
            sub = W[:, m * 128:(m + 1) * 128]
            t = np.ascontiguousarray(sub.reshape(NJ, 128, 128).transpose(1, 0, 2)).reshape(128, NJ * 128)
        elif kind == "conv":
            c = idx
            cols = np.concatenate([1024 + c * 128 + r128, 2048 + c * 128 + r128, 0 + c * 128 + r128])
            t = _in_tile(w_mix_in, cols)
        elif kind == "vproj":
            cols = 5 * 1024 + idx * 512 + np.arange(512)
            t = _in_tile(w_mix_in, cols)
        elif kind == "head":
            hd = idx
            cols = np.concatenate([3 * 1024 + hd * 128 + r128, 4 * 1024 + hd * 128 + r128, 6 * 1024 + hd * 128 + r128])
            t = _in_tile(w_mix_in, cols)
        elif kind == "merge":
            m = idx
            mc = m * 128 + r128
            t = np.concatenate([
                _in_tile(w_conv_out, mc).reshape(128, 8, 128),
                _in_tile(w_hg_out, mc).reshape(128, 8, 128),
                _in_tile(w_mix_in, 7 * 1024 + mc).reshape(128, 8, 128),
                _in_tile(w_mix_in, 8 * 1024 + mc).reshape(128, 8, 128)], axis=2).reshape(128, 4096)
        elif kind == "mixout":
            cols = idx * 512 + np.arange(512)
            t = _in_tile(w_mix_out, cols)
        assert t.shape[1] == E, (kind, t.shape, E)
        wf[w, :, :E] = t
    return wf


def _run(inputs, NT, n_cores):
    f = lambda k: np.asarray(inputs[k], dtype=np.float32)
    x = f("x")
    c = f("c")
    w_ada = f("w_ada")[0]
    wada = np.ascontiguousarray(
        w_ada.reshape(8, 128, 36, 256).transpose(2, 1, 0, 3)).reshape(36, 128, 2048)
    wf = _layout_weights(f("w_ffn1_in")[0], f("w_ffn1_out")[0], f("w_mix_in")[0], f("w_conv_out")[0],
                         f("w_hg_out")[0], f("w_mix_out")[0], f("w_ffn2_in")[0], f("w_ffn2_out")[0])
    ng = f("norm_gains")[0]
    cw = f("conv_w")[0]
    lbl = f("lb_logits")
    common = [_pc(f("b_ada")[0])] + [_pc(ng[n]) for n in range(6)] + [_pc(cw[j]) for j in range(3)] + \
             [_pc(f("conv_b")[0]), _pc(f("hg_norm_g")[0]), _pc(lbl[0]), _pc(lbl[1])]
    ident = np.eye(128, dtype=np.float32)
    masku = (np.arange(64)[:, None] <= np.arange(64)[None, :]).astype(np.int32)
    in_maps = []
    for b in range(n_cores):
        vec = np.ascontiguousarray(np.concatenate([_pc(c[b])] + common, axis=1)).astype(np.float32)
        assert vec.shape == (128, NV)
        in_maps.append({"x": np.ascontiguousarray(x[b]), "vec": vec, "wada": wada, "wf": wf,
                        "ident": ident, "masku": masku})
    nc = build_program(NT)
    res = run_bass_kernel_spmd(nc, in_maps, core_ids=list(range(n_cores)))
    return np.stack([np.asarray(r["y"], dtype=np.float32) for r in res.results], axis=0)


def kernel(**inputs):
    x = inputs["x"]
    B, S, _ = x.shape
    assert S % TT == 0
    return _run(inputs, S // TT, B)
```

```python
from contextlib import ExitStack
import numpy as np
import concourse.bass as bass
import concourse.mybir as mybir
from concourse.bass_utils import run_bass_kernel_spmd

F32 = mybir.dt.float32
BF16 = mybir.dt.bfloat16
I32 = mybir.dt.int32
ALU = mybir.AluOpType
AF = mybir.ActivationFunctionType

SAME_ENGINE_SYNC = True

D = 1024
DFF = 2816
NJ = DFF // 128
TT = 512
NS = 3
EPS = 1e-6
NV = 184
NW = 66
GH = 2


class Buf:
    __slots__ = ("name", "w", "r", "rd", "dma_sem", "dma_cnt", "excl")

    def __init__(self, name, excl=False):
        self.name = name
        self.excl = excl
        self.w = None
        self.r = {}
        self.rd = []
        self.dma_sem = None
        self.dma_cnt = 0


class Prog:
    ENGS = ("pe", "act", "dve", "pool", "sp")

    def __init__(self, nc, stack):
        self.nc = nc
        self.stack = stack
        self.ops = {e: [] for e in self.ENGS}
        self.seen = {e: {} for e in self.ENGS}
        self.seen_dma = {e: {} for e in self.ENGS}
        self.sems = {e: stack.enter_context(nc.semaphore("s_" + e)) for e in self.ENGS}
        self.nsem = 0
        self.phase = ""

    def _need(self, eng, tok, waits):
        if tok is None:
            return
        if tok[0] == "e":
            _, src, idx = tok
            if src == eng and (eng == "pe" or not SAME_ENGINE_SYNC):
                return
            if self.seen[eng].get(src, -1) >= idx:
                return
            self.seen[eng][src] = idx
            self.ops[src][idx][2] = True
            waits.append(tok)
        else:
            _, buf, cnt = tok
            if self.seen_dma[eng].get(buf.name, 0) >= cnt:
                return
            self.seen_dma[eng][buf.name] = cnt
            waits.append(tok)

    def _deps(self, eng, reads, writes):
        waits = []
        for b in reads:
            self._need(eng, b.w, waits)
        for b in writes:
            self._need(eng, b.w, waits)
            for t in b.r.values():
                self._need(eng, t, waits)
            for t in b.rd:
                self._need(eng, t, waits)
        return waits

    def _commit(self, tok, reads, writes):
        if tok[0] == "e":
            for b in reads:
                b.r[tok[1]] = tok
        else:
            for b in reads:
                b.rd.append(tok)
        for b in writes:
            b.w = tok
            b.r = {}
            b.rd = []

    def op(self, eng, fn, reads=(), writes=()):
        if any(b.excl for b in reads):
            writes = list(writes) + [b for b in reads if b.excl]
            reads = [b for b in reads if not b.excl]
        waits = self._deps(eng, reads, writes)
        idx = len(self.ops[eng])
        self.ops[eng].append([fn, waits, False, None, self.phase])
        tok = ("e", eng, idx)
        self._commit(tok, reads, writes)
        return tok

    def dma(self, eng, fn, reads=(), writes=(), sem_buf=None):
        waits = self._deps(eng, reads, writes)
        if sem_buf.dma_sem is None:
            self.nsem += 1
            sem_buf.dma_sem = self.stack.enter_context(self.nc.semaphore("d_%d" % self.nsem))
        sem_buf.dma_cnt += 1
        tok = ("d", sem_buf, sem_buf.dma_cnt)
        self.ops[eng].append([fn, waits, False, sem_buf, self.phase])
        self._commit(tok, reads, writes)
        return tok

    def emit(self, final_toks):
        nc = self.nc
        final_waits = []
        for t in final_toks:
            self._need("sp", t, final_waits)
        val = {}
        for e in self.ENGS:
            c = 0
            for i, o in enumerate(self.ops[e]):
                if o[2]:
                    c += 1
                    val[(e, i)] = c
        sems = self.sems

        def do_wait(eng, t):
            if t[0] == "e":
                eng.wait_ge(sems[t[1]], val[(t[1], t[2])])
            else:
                eng.wait_ge(t[1].dma_sem, 16 * t[2])

        def run(ename, eng):
            for o in self.ops[ename]:
                for t in o[1]:
                    do_wait(eng, t)
                ins = o[0](eng)
                if o[3] is not None:
                    ins.then_inc(o[3].dma_sem, 16)
                elif o[2]:
                    ins.then_inc(sems[ename], 1)
            if ename == "sp":
                for t in final_waits:
                    do_wait(eng, t)

        with nc.Block() as block:
            @block.tensor
            def _(e):
                run("pe", e)

            @block.scalar
            def _(e):
                run("act", e)

            @block.vector
            def _(e):
                run("dve", e)

            @block.gpsimd
            def _(e):
                run("pool", e)

            @block.sync
            def _(e):
                run("sp", e)


def wtile_sizes():
    sizes = []
    for _ in range(2):
        pass
    order = []
    order += [("ffn1_in", g, 4096) for g in range(11)]
    order += [("ffn1_out", m, 2816) for m in range(8)]
    order += [("vproj", h, 4096) for h in range(2)]
    order += [("head", h, 3072) for h in range(8)]
    order += [("conv", c, 3072) for c in range(8)]
    order += [("merge", m, 4096) for m in range(8)]
    order += [("mixout", h, 4096) for h in range(2)]
    order += [("ffn2_in", g, 4096) for g in range(11)]
    order += [("ffn2_out", m, 2816) for m in range(8)]
    assert len(order) == NW
    return order


WORDER = wtile_sizes()


STOP = 9
LAST_PROG = None


def build_program(NT):
    nc = bass.Bass("TRN2", target_bir_lowering=False)
    SEQ = NT * TT
    x_d = nc.dram_tensor("x", [SEQ, D], F32, kind="ExternalInput").ap()
    vec_d = nc.dram_tensor("vec", [128, NV], F32, kind="ExternalInput").ap()
    wada_d = nc.dram_tensor("wada", [36, 128, 2048], F32, kind="ExternalInput").ap()
    wf_d = nc.dram_tensor("wf", [NW, 128, 4096], F32, kind="ExternalInput").ap()
    ident_d = nc.dram_tensor("ident", [128, 128], F32, kind="ExternalInput").ap()
    mask_d = nc.dram_tensor("masku", [64, 64], I32, kind="ExternalInput").ap()
    y_d = nc.dram_tensor("y", [SEQ, D], F32, kind="ExternalOutput").ap()
    wb_d = nc.dram_tensor("wb", [NW, 128, 4096], BF16, kind="Internal").ap()

    st = ExitStack()
    with st:
        P = Prog(nc, st)

        def sb(name, shape, dt):
            return st.enter_context(nc.sbuf_tensor("sb_" + name, shape, dt))

        ident_f = sb("ident_f", [128, 128], F32)
        ident_b = sb("ident_b", [128, 128], BF16)
        ones_b = sb("ones_b", [128, 128], BF16)
        mask_sb = sb("mask_sb", [64, 64], I32)
        negh = sb("negh", [128, 8], F32)
        epsv = sb("epsv", [128, 8], F32)
        rmask = sb("rmask", [128, TT], F32)
        vec = sb("vec", [128, NV], F32)
        cact = sb("cact", [128, 8], F32)
        ada = sb("ada", [128, 72], F32)
        gp = sb("gp", [128, 24], F32)
        gg = sb("gg", [128, 24], F32)
        lbv = sb("lbv", [128, 40], F32)
        S = sb("S", [128, 8, 128], F32)
        Sbf = sb("Sbf", [128, 8, 128], BF16)
        Sdl = sb("Sdl", [128, 8, 128], F32)
        hist = sb("hist", [128, 8, 2], F32)
        dmid = sb("dmid", [128, 8, 8], F32)
        dl = sb("dl", [128, 8, 8], F32)
        dlm = sb("dlm", [128, 8, 8], F32)
        dtmp = sb("dtmp", [128, 8, 8], F32)
        scs = [sb("scs%d" % i, [64, 64], BF16) for i in range(4)]
        xin_t = [sb("xin%d" % i, [128, D], F32) for i in range(2)]
        xout_t = [sb("xout%d" % i, [128, D], F32) for i in range(2)]
        xT = sb("xT", [128, 8, TT], F32)
        xT1 = sb("xT1", [128, 8, TT], F32)
        rstd2 = sb("rstd2", [128, TT], F32)
        hT = sb("hT", [128, 8, TT], BF16)
        yT = sb("yT", [128, 8, TT], F32)
        sq = [sb("sq%d" % i, [128, TT], BF16) for i in range(3)]
        TR = [sb("tr%d" % i, [128, TT], F32) for i in range(3)]
        ms = sb("ms", [128, TT], F32)
        rstd = sb("rstd", [128, TT], F32)
        AR = sb("AR", [128, 24, TT], BF16)
        vT = sb("vT", [128, 8, TT], BF16)
        gsil = sb("gsil", [128, 8, TT], BF16)
        ogT = sb("ogT", [128, 8, TT], BF16)
        mT = sb("mT", [128, 8, TT], BF16)
        ccvw = [sb("ccvw%d" % i, [128, TT + 2], F32) for i in range(2)]
        ktok = sb("ktok", [64, GH, 8, 128], BF16)
        vtok = sb("vtok", [64, GH, 8, 128], BF16)
        wslot = [sb("wslot%d" % i, [128, 4096], BF16) for i in range(NS)]
        wstg = [sb("wstg%d" % i, [128, 1024], F32) for i in range(3)]

        banks = [st.enter_context(nc.psum_tensor("bank%d" % i, [128, 512], F32)) for i in range(8)]

        def bl(prefix, n, excl=False):
            return [Buf("%s%d" % (prefix, i), excl) for i in range(n)]

        bCONST = Buf("const")
        bVEC = Buf("vec")
        bXT, bHT, bYT = bl("xT", 8), bl("hT", 8), bl("yT", 8)
        bAR = bl("AR", 24)
        bVT, bGS, bOG, bMT = bl("vT", 8), bl("gs", 8), bl("og", 8), bl("mT", 8)
        bSQ, bTR = bl("sq", 3), bl("tr", 3)
        bMS, bRSTD = Buf("ms"), Buf("rstd")
        bWARM = Buf("warm")
        bRSTD2 = Buf("rstd2")
        bXT1 = bl("xU", 8)
        xTb, bXTb = [xT, xT1], [bXT, bXT1]
        cur = {"X": xT, "bX": bXT}
        bBANK = bl("bank", 8, True)
        bSS = bBANK[4]
        RING = [0, 1, 2, 3, 5, 6, 7]
        bSCS = bl("scs", 4)
        bS, bSbf, bSdl = bl("S", 8), bl("Sbf", 8), bl("Sdl", 8)
        bHIST = bl("hist", 8)
        bDEC = bl("dec", 8)
        bDTMP = Buf("dtmp")
        bKTOK, bVTOK = bl("ktok", GH), bl("vtok", GH)
        bXIN, bXOUT = bl("xin", 2), bl("xout", 2)
        bWS = bl("ws", NS)
        bWSTG = bl("wstg", 3)
        bCCVW = bl("ccvw", 2)
        bWB = [Buf("wb%d" % w) for w in range(NW)]
        bADA = bSS

        rr = {"wstg": 0, "bank": 0, "tr": 0, "sq": 0, "sc": 0, "kv": 0, "scs": 0, "ccvw": 0, "xin": 0, "xout": 0}

        def nxt(kind, n):
            i = rr[kind]
            rr[kind] = (i + 1) % n
            return i

        def bank():
            i = RING[nxt("bank", 7)]
            return banks[i], bBANK[i]

        def tr():
            i = nxt("tr", 3)
            return TR[i], bTR[i]

        def act_(out, in_, func, R, W, **kw):
            return P.op("act", lambda e: e.activation(out=out, in_=in_, func=func, **kw), reads=R, writes=W)

        def tt_(eng, out, a, b, op, R, W):
            return P.op(eng, lambda e: e.tensor_tensor(out=out, in0=a, in1=b, op=op), reads=R, writes=W)

        def ts_(eng, out, a, s1, s2, op0, op1, R, W):
            return P.op(eng, lambda e: e.tensor_scalar(out=out, in0=a, scalar1=s1, scalar2=s2, op0=op0, op1=op1),
                        reads=R, writes=W)

        def stt_(out, a, s, b, op0, op1, R, W):
            return P.op("dve", lambda e: e.scalar_tensor_tensor(out=out, in0=a, scalar=s, in1=b, op0=op0, op1=op1),
                        reads=R, writes=W)

        def copy_(eng, out, in_, R, W):
            if eng == "act":
                return act_(out, in_, AF.Copy, R, W)
            return P.op(eng, lambda e: e.tensor_copy(out=out, in_=in_), reads=R, writes=W)

        def mm_(out, lhsT, rhs, start, stop, R, W):
            return P.op("pe", lambda e: e.matmul(out, lhsT=lhsT, rhs=rhs, start=start, stop=stop), reads=R, writes=W)

        def tp_(out, in_, ident, R, W):
            return P.op("pe", lambda e: e.transpose(out=out, in_=in_, identity=ident), reads=R, writes=W)

        def memset_(eng, ap, v, W):
            return P.op(eng, lambda e: e.memset(ap, v), writes=W)

        P.dma("sp", lambda e: e.dma_start(out=vec[:], in_=vec_d[:, :]), writes=[bVEC], sem_buf=bVEC)
        bID = Buf("identf")
        bMK = Buf("mask")
        P.dma("sp", lambda e: e.dma_start(out=ident_f[:], in_=ident_d[:, :]), writes=[bID], sem_buf=bID)
        P.dma("sp", lambda e: e.dma_start(out=mask_sb[:], in_=mask_d[:, :]), writes=[bMK], sem_buf=bMK)
        copy_("dve", ident_b[:], ident_f[:], [bID], [bCONST])
        memset_("pool", ones_b[:], 1.0, [bCONST])
        memset_("pool", negh[:], -0.5, [bCONST])
        memset_("pool", epsv[:], EPS, [bCONST])
        memset_("pool", rmask[:], 1.0, [bCONST])
        memset_("pool", rmask[:].rearrange("p (c t) -> p c t", t=64)[:, :, 0:1], 0.0, [bCONST])
        memset_("pool", S[:], 0.0, bS)
        memset_("pool", hist[:], 0.0, bHIST)
        for i in range(4):
            memset_("pool", scs[i][:], 0.0, [bSCS[i]])

        V_C, V_BADA, V_NG, V_CW, V_CB, V_HGN, V_LB = 0, 8, 80, 128, 152, 160, 168
        act_(cact[:], vec[:, V_C:V_C + 8], AF.Silu, [bVEC], [bCONST])
        for i in range(36):
            k = i % 2
            stg2 = yT[:, 4 * k:4 * k + 4, :].rearrange("p a b -> p (a b)")
            stg = stg2.rearrange("p (c n) -> p c n", c=8)
            sbufs = bYT[4 * k:4 * k + 4]
            P.dma("sp", lambda e, stg2=stg2, i=i: e.dma_start(out=stg2, in_=wada_d[i, :, :]),
                  writes=sbufs, sem_buf=sbufs[0])
            for nb in range(2):
                blk = i * 2 + nb
                for c in range(8):
                    mm_(banks[4][:, blk:blk + 1], stg[:, c, nb * 128:(nb + 1) * 128], cact[:, c:c + 1],
                        c == 0, c == 7, sbufs + [bCONST], [bADA])
        tt_("dve", ada[:], banks[4][:, 0:72], vec[:, V_BADA:V_BADA + 72], ALU.add, [bADA, bVEC], [bCONST])
        RW = [0.5, 1.0, 0.5]
        for i in range(3):
            sh_c, sc_c, g_c = (3 * i) * 8, (3 * i + 1) * 8, (3 * i + 2) * 8
            stt_(gp[:, i * 8:(i + 1) * 8], ada[:, sc_c:sc_c + 8], 1.0,
                 vec[:, V_NG + (2 * i) * 8:V_NG + (2 * i) * 8 + 8], ALU.add, ALU.mult, [bCONST, bVEC], [bCONST])
            stt_(gg[:, i * 8:(i + 1) * 8], ada[:, g_c:g_c + 8], RW[i],
                 vec[:, V_NG + (2 * i + 1) * 8:V_NG + (2 * i + 1) * 8 + 8], ALU.mult, ALU.mult, [bCONST, bVEC], [bCONST])

        def shv(i, c):
            return ada[:, (3 * i) * 8 + c:(3 * i) * 8 + c + 1]

        tt_("dve", lbv[:, 32:40], vec[:, V_LB:V_LB + 8], vec[:, V_LB + 8:V_LB + 16], ALU.subtract, [bVEC], [bCONST])
        act_(lbv[:, 0:8], lbv[:, 32:40], AF.Sigmoid, [bCONST], [bCONST])
        ts_("dve", lbv[:, 8:16], lbv[:, 0:8], 0.5, 0.5, ALU.mult, ALU.add, [bCONST], [bCONST])
        ts_("dve", lbv[:, 16:24], lbv[:, 0:8], -0.5, 0.5, ALU.mult, ALU.add, [bCONST], [bCONST])
        ts_("dve", lbv[:, 24:32], lbv[:, 0:8], 0.5, -0.5, ALU.mult, ALU.add, [bCONST], [bCONST])

        def _en(kind):
            if kind.startswith("ffn1"):
                return STOP >= 4
            if kind.startswith("ffn2"):
                return STOP >= 6
            return STOP >= 5
        wseq = [w for _ in range(NT) for w in range(NW) if _en(WORDER[w][0])]
        ws = {"issued": 0, "used": 0}

        n_first = len(wseq) // NT

        def w_issue():
            n = ws["issued"]
            w = wseq[n]
            E = WORDER[w][2]
            s = n % NS
            if n < n_first:
                for off in range(0, E, 1024):
                    m = min(1024, E - off)
                    k = nxt("wstg", 3)
                    P.dma("sp", lambda e, k=k, off=off, m=m: e.dma_start(out=wstg[k][:, 0:m], in_=wf_d[w, :, off:off + m]),
                          writes=[bWSTG[k]], sem_buf=bWSTG[k])
                    ceng = ("pool", "dve", "pool", "act")[(off // 1024) % 4]
                    copy_(ceng, wslot[s][:, off:off + m], wstg[k][:, 0:m], [bWSTG[k]], [bWS[s]])
                P.dma("pool", lambda e: e.dma_start(out=wb_d[w, :, 0:E], in_=wslot[s][:, 0:E]),
                      reads=[bWS[s]], writes=[bWB[w]], sem_buf=bWB[w])
            else:
                P.dma("sp", lambda e: e.dma_start(out=wslot[s][:, 0:E], in_=wb_d[w, :, 0:E]),
                      reads=[bWB[w]], writes=[bWS[s]], sem_buf=bWS[s])
            ws["issued"] = n + 1

        def w_next(kind):
            n = ws["used"]
            assert WORDER[wseq[n]][0] == kind, (WORDER[wseq[n]], kind)
            while ws["issued"] < min(len(wseq), n + NS):
                w_issue()
            ws["used"] = n + 1
            s = n % NS
            return wslot[s], bWS[s]

        def rms_stats_finish(inv_n, ss_ap=None, bss=None, rs=None, brs=None):
            if ss_ap is None:
                ss_ap, bss, rs, brs = banks[4][:, :], bSS, rstd, bRSTD
                m_, bm_ = ms, bMS
            else:
                m_, bm_ = tr()
            act_(m_[:], ss_ap, AF.Ln, [bss, bCONST], [bm_], scale=inv_n, bias=epsv[:, 0:1])
            act_(rs[:], m_[:], AF.Exp, [bm_], [brs], scale=-0.5)

        def prenorm(i, alt=False):
            P.phase = "prenorm%d" % i
            X, bX = cur["X"], cur["bX"]
            if alt:
                ss_t, bss = bank()
                ss_ap, rs, brs = ss_t[:, :], rstd2, bRSTD2
            else:
                ss_ap, bss, rs, brs = banks[4][:, :], bSS, rstd, bRSTD
            for c in range(8):
                r = nxt("sq", 3)
                act_(sq[r][:], X[:, c, :], AF.Square, [bX[c]], [bSQ[r]])
                mm_(ss_ap, ones_b[:], sq[r][:], c == 0, c == 7, [bCONST, bSQ[r]], [bss])
            rms_stats_finish(1.0 / D, ss_ap, bss, rs, brs)
            for c in range(8):
                t, bt_ = tr()
                tt_("dve", t[:], X[:, c, :], rs[:], ALU.mult, [bX[c], brs], [bt_])
                act_(hT[:, c, :], t[:], AF.Identity, [bt_, bCONST], [bHT[c]],
                     scale=gp[:, i * 8 + c:i * 8 + c + 1], bias=shv(i, c))

        def y_block_done(m, y_ps, by):
            copy_("dve", yT[:, m, :], y_ps[:, :], [by], [bYT[m]])
            r = nxt("sq", 3)
            act_(sq[r][:], yT[:, m, :], AF.Square, [bYT[m]], [bSQ[r]])
            return r

        def postnorm(i):
            P.phase = "postnorm%d" % i
            X, bX = cur["X"], cur["bX"]
            rms_stats_finish(1.0 / D)
            for c in range(8):
                t, bt_ = tr()
                tt_("dve", t[:], yT[:, c, :], rstd[:], ALU.mult, [bYT[c], bRSTD], [bt_])
                stt_(X[:, c, :], t[:], gg[:, i * 8 + c:i * 8 + c + 1], X[:, c, :], ALU.mult, ALU.add,
                     [bt_, bCONST, bX[c]], [bX[c]])

        def out_blocks(kind, nblk_per_tile, ntiles, rhs_of, rbufs_of, nk, hook=None):
            act_(negh[:, 0:1], epsv[:, 0:1], AF.Ln, [bCONST], [bWARM])
            pend = None
            for wt_i in range(ntiles):
                wt, bw = w_next(kind)
                wv = wt[:, 0:nk * nblk_per_tile * 128].rearrange("p (k n) -> p k n", k=nk)
                for mm in range(nblk_per_tile):
                    m = wt_i * nblk_per_tile + mm
                    y_ps, by = bank()
                    for k in range(nk):
                        mm_(y_ps[:, :], wv[:, k, mm * 128:(mm + 1) * 128], rhs_of(k), k == 0, k == nk - 1,
                            [bw] + rbufs_of(k), [by])
                    if pend is not None:
                        mm_(banks[4][:, :], ones_b[:], sq[pend[0]][:], pend[1] == 0, False,
                            [bCONST, bSQ[pend[0]]], [bSS])
                    r = y_block_done(m, y_ps, by)
                    pend = (r, m)
                    if hook is not None and m == 1:
                        mm_(banks[4][:, :], ones_b[:], sq[pend[0]][:], pend[1] == 0, False,
                            [bCONST, bSQ[pend[0]]], [bSS])
                        pend = None
                        hook()
            mm_(banks[4][:, :], ones_b[:], sq[pend[0]][:], False, True, [bCONST, bSQ[pend[0]]], [bSS])

        def ffn(i, kin, kout, pre=True, mid_hook=None, out_hook=None):
            if pre:
                prenorm(i)
            for g in range(11):
                P.phase = "ffn%d_in" % i
                if mid_hook is not None and g == 2:
                    mid_hook()
                    P.phase = "ffn%d_in" % i
                wt, bw = w_next(kin)
                wv = wt[:, :].rearrange("p (c n) -> p c n", c=8)
                for jj in range(2):
                    j = 2 * g + jj
                    a_ps, ba_ = bank()
                    b_ps, bb_ = bank()
                    for c in range(8):
                        mm_(a_ps[:, :], wv[:, c, jj * 128:(jj + 1) * 128], hT[:, c, :], c == 0, c == 7,
                            [bw, bHT[c]], [ba_])
                    for c in range(8):
                        mm_(b_ps[:, :], wv[:, c, 256 + jj * 128:256 + (jj + 1) * 128], hT[:, c, :], c == 0, c == 7,
                            [bw, bHT[c]], [bb_])
                    t, bt_ = tr()
                    act_(t[:], a_ps[:, :], AF.Silu, [ba_], [bt_])
                    tt_("dve", AR[:, j, :], t[:], b_ps[:, :], ALU.mult, [bt_, bb_], [bAR[j]])
            P.phase = "ffn%d_out" % i
            def oh():
                out_hook()
                P.phase = "ffn%d_out" % i
            out_blocks(kout, 1, 8, lambda k: AR[:, k, :], lambda k: [bAR[k]], NJ, hook=(oh if out_hook else None))
            postnorm(i)

        def mixer():
            i = 1
            prenorm(i)
            P.phase = "vproj"
            for hg_ in range(2):
                wt, bw = w_next("vproj")
                wv = wt[:, :].rearrange("p (c n) -> p c n", c=8)
                for hh in range(4):
                    hd = hg_ * 4 + hh
                    v_ps, bv_ = bank()
                    for k in range(8):
                        mm_(v_ps[:, :], wv[:, k, hh * 128:(hh + 1) * 128], hT[:, k, :], k == 0, k == 7,
                            [bw, bHT[k]], [bv_])
                    copy_("act" if hh % 2 == 0 else "dve", vT[:, hd, :], v_ps[:, :], [bv_], [bVT[hd]])
            qT = lambda hd: AR[:, 8 + hd, :]
            kT = lambda hd: AR[:, 16 + hd, :]
            P.phase = "heads"

            def TY(i):
                return yT[:, i, :], bYT[i]

            for pr in range(4):
                hds = [2 * pr, 2 * pr + 1]
                proj = {}
                for hi_, hd in enumerate(hds):
                    wt, bw = w_next("head")
                    wv = wt[:, 0:3072].rearrange("p (c n) -> p c n", c=8)
                    q_ps, bqp = bank()
                    f_ps, bfp = bank()
                    g_ps, bgp = bank()
                    for (ps_, bp_, o_) in ((q_ps, bqp, 0), (f_ps, bfp, 128), (g_ps, bgp, 256)):
                        for k in range(8):
                            mm_(ps_[:, :], wv[:, k, o_:o_ + 128], hT[:, k, :], k == 0, k == 7, [bw, bHT[k]], [bp_])
                    proj[hd] = (q_ps, bqp, f_ps, bfp, g_ps, bgp)
                for hi_, hd in enumerate(hds):
                    (q_ps, bqp, f_ps, bfp, g_ps, bgp) = proj[hd]
                    T1, b1 = TY(4 * hi_)
                    act_(qT(hd), q_ps[:, :], AF.Silu, [bqp], [bAR[8 + hd]])
                    act_(T1, f_ps[:, :], AF.Tanh, [bfp], [b1], scale=0.5)
                    act_(gsil[:, hd, :], g_ps[:, :], AF.Silu, [bgp], [bGS[hd]])
                for hi_, hd in enumerate(hds):
                    T1, b1 = TY(4 * hi_)
                    T2, b2 = TY(4 * hi_ + 1)
                    c0 = lbv[:, 8 + hd:9 + hd]
                    c1 = lbv[:, 16 + hd:17 + hd]
                    nc1 = lbv[:, 24 + hd:25 + hd]
                    ts_("pool", T2, T1, nc1, c1, ALU.mult, ALU.add, [b1, bCONST], [b2])
                    ts_("dve", T1, T1, c1, c0, ALU.mult, ALU.add, [b1, bCONST], [b1])
                for hi_, hd in enumerate(hds):
                    T1, b1 = TY(4 * hi_)
                    act_(T1, T1, AF.Ln, [b1], [b1])
                for hi_, hd in enumerate(hds):
                    T1, b1 = TY(4 * hi_)
                    T3, b3 = TY(4 * hi_ + 2)
                    P.op("dve", lambda e, T1=T1, T3=T3: e.tensor_tensor_scan(out=T3, data0=rmask[:], data1=T1, initial=0.0,
                                                                            op0=ALU.mult, op1=ALU.add),
                         reads=[bCONST, b1], writes=[b3])
                    a3 = T3.rearrange("p (c t) -> p c t", t=64)
                    tt_("dve", dtmp[:, hd, :], a3[:, :, 63], a3[:, :, 31], ALU.subtract, [b3], [bDTMP])
                for hi_, hd in enumerate(hds):
                    T3, b3 = TY(4 * hi_ + 2)
                    a3 = T3.rearrange("p (c t) -> p c t", t=64)
                    act_(dmid[:, hd, :], a3[:, :, 31], AF.Exp, [b3], [bDEC[hd]])
                    act_(dl[:, hd, :], a3[:, :, 63], AF.Exp, [b3], [bDEC[hd]])
                    act_(dlm[:, hd, :], dtmp[:, hd, :], AF.Exp, [bDTMP], [bDEC[hd]])
                for hi_, hd in enumerate(hds):
                    T1, b1 = TY(4 * hi_)
                    T2, b2 = TY(4 * hi_ + 1)
                    T3, b3 = TY(4 * hi_ + 2)
                    T4, b4 = TY(4 * hi_ + 3)
                    a3 = T3.rearrange("p (c t) -> p c t", t=64)
                    E3 = T4.rearrange("p (c t) -> p c t", t=64)
                    tt_("dve", E3, a3[:, :, 31:32].to_broadcast([128, 8, 64]), a3, ALU.subtract, [b3], [b4])
                    act_(T1, T4, AF.Exp, [b4], [b1])
                    act_(T4, T4, AF.Exp, [b4], [b4], scale=-1.0)
                    tt_("dve", kT(hd), T2, T1, ALU.mult, [b2, b1], [bAR[16 + hd]])
                    tt_("pool", qT(hd), qT(hd), T4, ALU.mult, [bAR[8 + hd], b4], [bAR[8 + hd]])

            P.phase = "conv"
            for c in range(8):
                wt, bw = w_next("conv")
                wv = wt[:, 0:3072].rearrange("p (c n) -> p c n", c=8)
                cc_ps, bcc = bank()
                cv_ps, bcv = bank()
                cb_ps, bcb = bank()
                for (ps_, bp_, o_) in ((cc_ps, bcc, 0), (cv_ps, bcv, 128), (cb_ps, bcb, 256)):
                    for k in range(8):
                        mm_(ps_[:, :], wv[:, k, o_:o_ + 128], hT[:, k, :], k == 0, k == 7, [bw, bHT[k]], [bp_])
                t, bt_ = tr()
                copy_("act", t[:], cv_ps[:, :], [bcv], [bt_])
                r = nxt("ccvw", 2)
                cw = ccvw[r]
                bcw = bCCVW[r]
                copy_("pool", cw[:, 0:2], hist[:, c, :], [bHIST[c]], [bcw])
                tt_("dve", cw[:, 2:TT + 2], cc_ps[:, :], t[:], ALU.mult, [bcc, bt_, bcw], [bcw])
                copy_("pool", hist[:, c, :], cw[:, TT:TT + 2], [bcw], [bHIST[c]])
                a_, ba2 = tr()
                w0 = vec[:, V_CW + 0 * 8 + c:V_CW + 0 * 8 + c + 1]
                w1 = vec[:, V_CW + 1 * 8 + c:V_CW + 1 * 8 + c + 1]
                w2 = vec[:, V_CW + 2 * 8 + c:V_CW + 2 * 8 + c + 1]
                cbv = vec[:, V_CB + c:V_CB + c + 1]
                ts_("dve", a_[:], cw[:, 2:TT + 2], w2, cbv, ALU.mult, ALU.add, [bcw, bVEC], [ba2])
                stt_(a_[:], cw[:, 1:TT + 1], w1, a_[:], ALU.mult, ALU.add, [bcw, bVEC, ba2], [ba2])
                stt_(a_[:], cw[:, 0:TT], w0, a_[:], ALU.mult, ALU.add, [bcw, bVEC, ba2], [ba2])
                tt_("dve", AR[:, c, :], a_[:], cb_ps[:, :], ALU.mult, [ba2, bcb], [bAR[c]])
            for gi in range(8 // GH):
                heads = list(range(gi * GH, (gi + 1) * GH))
                P.phase = "rec_tp"
                for hi_, hd in enumerate(heads):
                    for (src, bsrc, dst, bdst) in ((kT(hd), bAR[16 + hd], ktok, bKTOK[hi_]),
                                                   (vT[:, hd, :], bVT[hd], vtok, bVTOK[hi_])):
                        tp_ps, btp = bank()
                        psb = tp_ps[:, :].bitcast(BF16)
                        for ch in range(8):
                            tp_(psb[0:64, ch * 128:(ch + 1) * 128], src[:, ch * 64:(ch + 1) * 64], ident_b[:],
                                [bsrc, bCONST], [btp])
                        copy_("dve" if hi_ % 2 == 0 else "act",
                              dst[:, hi_, :, :].rearrange("p a b -> p (a b)"), psb[0:64, 0:1024], [btp], [bdst])
                obanks = [bank() for _ in heads]
                scb = [bank(), bank()]
                kvb = bank()
                P.phase = "rec"
                steps = [(ch, hi_, hd) for ch in range(8) for hi_, hd in enumerate(heads)]

                def issue_sc(n):
                    ch, hi_, hd = steps[n]
                    cs = slice(ch * 64, (ch + 1) * 64)
                    sc_t, bsc = scb[n % 2]
                    sc_ps = sc_t[0:64, 0:64]
                    mm_(sc_ps, kT(hd)[:, cs], qT(hd)[:, cs], True, True, [bAR[16 + hd], bAR[8 + hd]], [bsc])
                    ri = nxt("scs", 4)
                    P.op("dve", lambda e, ri=ri, sc_ps=sc_ps: e.copy_predicated(out=scs[ri][:], mask=mask_sb[:], data=sc_ps),
                         reads=[bsc, bMK], writes=[bSCS[ri]])
                    return ri

                ri_next = issue_sc(0)
                for n, (ch, hi_, hd) in enumerate(steps):
                    o_ps, bo = obanks[hi_]
                    cs = slice(ch * 64, (ch + 1) * 64)
                    ri = ri_next
                    act_(Sbf[:, hd, :], S[:, hd, :], AF.Identity, [bS[hd], bDEC[hd]], [bSbf[hd]],
                         scale=dmid[:, hd, ch:ch + 1])
                    ts_("dve", Sdl[:, hd, :], S[:, hd, :], dl[:, hd, ch:ch + 1], None, ALU.mult, ALU.bypass,
                        [bS[hd], bDEC[hd]], [bSdl[hd]])
                    if n + 1 < len(steps):
                        ri_next = issue_sc(n + 1)
                    mm_(o_ps[:, cs], vtok[:, hi_, ch, :], scs[ri][:], True, False, [bVTOK[hi_], bSCS[ri]], [bo])
                    mm_(o_ps[:, cs], Sbf[:, hd, :], qT(hd)[:, cs], False, True, [bSbf[hd], bAR[8 + hd]], [bo])
                    kv_ps = kvb[0][:, 0:128]
                    mm_(kv_ps, ktok[:, hi_, ch, :], vtok[:, hi_, ch, :], True, True, [bKTOK[hi_], bVTOK[hi_]], [kvb[1]])
                    stt_(S[:, hd, :], kv_ps, dlm[:, hd, ch:ch + 1], Sdl[:, hd, :], ALU.mult, ALU.add,
                         [kvb[1], bDEC[hd], bSdl[hd]], [bS[hd]])
                P.phase = "onorm"
                for hi_, hd in enumerate(heads):
                    o_ps, bo = obanks[hi_]
                    r = nxt("sq", 3)
                    act_(sq[r][:], o_ps[:, :], AF.Square, [bo], [bSQ[r]])
                    mm_(banks[4][:, :], ones_b[:], sq[r][:], True, True, [bCONST, bSQ[r]], [bSS])
                    rms_stats_finish(1.0 / 128)
                    t, bt_ = tr()
                    tt_("dve", t[:], o_ps[:, :], rstd[:], ALU.mult, [bo, bRSTD], [bt_])
                    stt_(ogT[:, hd, :], t[:], vec[:, V_HGN + hd:V_HGN + hd + 1], gsil[:, hd, :], ALU.mult, ALU.mult,
                         [bt_, bVEC, bGS[hd]], [bOG[hd]])
            P.phase = "merge"
            for m in range(8):
                wt, bw = w_next("merge")
                wv = wt[:, :].rearrange("p (c n) -> p c n", c=8)
                ya_ps, bya = bank()
                yb_ps, byb = bank()
                ga_ps, bga = bank()
                gb_ps, bgb = bank()
                for k in range(8):
                    mm_(ya_ps[:, :], wv[:, k, 0:128], AR[:, k, :], k == 0, k == 7, [bw, bAR[k]], [bya])
                for k in range(8):
                    mm_(yb_ps[:, :], wv[:, k, 128:256], ogT[:, k, :], k == 0, k == 7, [bw, bOG[k]], [byb])
                for k in range(8):
                    mm_(ga_ps[:, :], wv[:, k, 256:384], hT[:, k, :], k == 0, k == 7, [bw, bHT[k]], [bga])
                for k in range(8):
                    mm_(gb_ps[:, :], wv[:, k, 384:512], hT[:, k, :], k == 0, k == 7, [bw, bHT[k]], [bgb])
                o4 = 4 * (m % 2)
                Y0, y0 = yT[:, o4, :], bYT[o4]
                Y1, y1 = yT[:, o4 + 1, :], bYT[o4 + 1]
                Y2, y2 = yT[:, o4 + 2, :], bYT[o4 + 2]
                Y3, y3 = yT[:, o4 + 3, :], bYT[o4 + 3]
                act_(Y0, ga_ps[:, :], AF.Sigmoid, [bga], [y0])
                act_(Y1, gb_ps[:, :], AF.Sigmoid, [bgb], [y1])
                tt_("dve", Y2, Y0, ya_ps[:, :], ALU.mult, [y0, bya], [y2])
                tt_("dve", Y3, Y1, yb_ps[:, :], ALU.mult, [y1, byb], [y3])
                tt_("pool", mT[:, m, :], Y2, Y3, ALU.add, [y2, y3], [bMT[m]])
            P.phase = "mixout"
            out_blocks("mixout", 4, 2, lambda k: mT[:, k, :], lambda k: [bMT[k]], 8)
            postnorm(i)

        def load_x(it, s):
            r = nxt("xin", 2)
            row = it * TT + s * 128
            P.dma("sp", lambda e: e.dma_start(out=xin_t[r][:], in_=x_d[row:row + 128, :]),
                  writes=[bXIN[r]], sem_buf=bXIN[r])
            return r

        final = []
        xq = []

        def xin(it):
            P.phase = "xin"
            X, bX = cur["X"], cur["bX"]
            if it == 0:
                xq.append(load_x(0, 0))
                xq.append(load_x(0, 1))
            for s in range(4):
                r = xq.pop(0)
                for half in range(2):
                    tp_ps, btp = bank()
                    for cc in range(4):
                        c = half * 4 + cc
                        tp_(tp_ps[:, cc * 128:(cc + 1) * 128], xin_t[r][:, c * 128:(c + 1) * 128], ident_f[:],
                            [bXIN[r], bID], [btp])
                    copy_("act" if half == 0 else "dve",
                          X[:, half * 4:half * 4 + 4, s * 128:(s + 1) * 128],
                          tp_ps[:, :].rearrange("p (c t) -> p c t", c=4), [btp], bX[half * 4:half * 4 + 4])
                if s < 2:
                    xq.append(load_x(it, s + 2))
                elif it + 1 < NT:
                    xq.append(load_x(it + 1, s - 2))

        def xout(it, X, bX):
            P.phase = "xout"
            for s in range(4):
                r = nxt("xout", 2)
                for half in range(2):
                    tp_ps, btp = bank()
                    for cc in range(4):
                        c = half * 4 + cc
                        tp_(tp_ps[:, cc * 128:(cc + 1) * 128], X[:, c, s * 128:(s + 1) * 128], ident_f[:],
                            [bX[c], bID], [btp])
                    copy_("act" if half == 0 else "dve", xout_t[r][:, half * 512:(half + 1) * 512], tp_ps[:, :],
                          [btp], [bXOUT[r]])
                row = it * TT + s * 128
                final.append(P.dma("act", lambda e, r=r, row=row: e.dma_start(out=y_d[row:row + 128, :], in_=xout_t[r][:]),
                                   reads=[bXOUT[r]], sem_buf=bXOUT[r]))

        def prep(it, alt):
            cur["X"], cur["bX"] = xTb[it % 2], bXTb[it % 2]
            xin(it)
            prenorm(0, alt=alt)

        prep(0, False)
        pend_xout = None
        for it in range(NT):
            cur["X"], cur["bX"] = xTb[it % 2], bXTb[it % 2]

            def out_hook(it=it):
                if it + 1 < NT:
                    prep(it + 1, True)
                    cur["X"], cur["bX"] = xTb[it % 2], bXTb[it % 2]

            hooked = False
            if STOP >= 4:
                ffn(0, "ffn1_in", "ffn1_out", pre=False, mid_hook=pend_xout)
                pend_xout = None
            if pend_xout is not None:
                pend_xout()
                pend_xout = None
            if STOP >= 5:
                mixer()
            if STOP >= 6:
                ffn(2, "ffn2_in", "ffn2_out", out_hook=out_hook)
            else:
                out_hook()
            pend_xout = (lambda it=it, X=cur["X"], bX=cur["bX"]: xout(it, X, bX))
        pend_xout()
        P.emit(final)
        global LAST_PROG
        LAST_PROG = P
    return nc


def _pc(v):
    return np.ascontiguousarray(v.reshape(-1, 128).T)


def _in_tile(W, cols):
    sub = W[:, cols]
    n = sub.shape[1]
    return np.ascontiguousarray(sub.reshape(8, 128, n).transpose(1, 0, 2)).reshape(128, 8 * n)


def _layout_weights(w_ffn1_in, w_ffn1_out, w_mix_in, w_conv_out, w_hg_out, w_mix_out, w_ffn2_in, w_ffn2_out):
    wf = np.zeros((NW, 128, 4096), np.float32)
    r128 = np.arange(128)
    for w, (kind, idx, E) in enumerate(WORDER):
        if kind in ("ffn1_in", "ffn2_in"):
            W = w_ffn1_in if kind == "ffn1_in" else w_ffn2_in
            g = idx
            cols = np.concatenate([(2 * g) * 128 + r128, (2 * g + 1) * 128 + r128,
                                   DFF + (2 * g) * 128 + r128, DFF + (2 * g + 1) * 128 + r128])
            t = _in_tile(W, cols)
        elif kind in ("ffn1_out", "ffn2_out"):
            W = w_ffn1_out if kind == "ffn1_out" else w_ffn2_out
            m = idx
            sub = W[:, m * 128:(m + 1) * 128]
            t = np.ascontiguousarray(sub.reshape(NJ, 128, 128).transpose(1, 0, 2)).reshape(128, NJ * 128)
        elif kind == "conv":
            c = idx
            cols = np.concatenate([1024 + c * 128 + r128, 2048 + c * 128 + r128, 0 + c * 128 + r128])
            t = _in_tile(w_mix_in, cols)
        elif kind == "vproj":
            cols = 5 * 1024 + idx * 512 + np.arange(512)
            t = _in_tile(w_mix_in, cols)
        elif kind == "head":
            hd = idx
            cols = np.concatenate([3 * 1024 + hd * 128 + r128, 4 * 1024 + hd * 128 + r128, 6 * 1024 + hd * 128 + r128])
            t = _in_tile(w_mix_in, cols)
        elif kind == "merge":
            m = idx
            mc = m * 128 + r128
            t = np.concatenate([
                _in_tile(w_conv_out, mc).reshape(128, 8, 128),
                _in_tile(w_hg_out, mc).reshape(128, 8, 128),
                _in_tile(w_mix_in, 7 * 1024 + mc).reshape(128, 8, 128),
                _in_tile(w_mix_in, 8 * 1024 + mc).reshape(128, 8, 128)], axis=2).reshape(128, 4096)
        elif kind == "mixout":
            cols = idx * 512 + np.arange(512)
            t = _in_tile(w_mix_out, cols)
        assert t.shape[1] == E, (kind, t.shape, E)
        wf[w, :, :E] = t
    return wf


def _run(inputs, NT, n_cores):
    f = lambda k: np.asarray(inputs[k], dtype=np.float32)
    x = f("x")
    c = f("c")
    w_ada = f("w_ada")[0]
    wada = np.ascontiguousarray(
        w_ada.reshape(8, 128, 36, 256).transpose(2, 1, 0, 3)).reshape(36, 128, 2048)
    wf = _layout_weights(f("w_ffn1_in")[0], f("w_ffn1_out")[0], f("w_mix_in")[0], f("w_conv_out")[0],
                         f("w_hg_out")[0], f("w_mix_out")[0], f("w_ffn2_in")[0], f("w_ffn2_out")[0])
    ng = f("norm_gains")[0]
    cw = f("conv_w")[0]
    lbl = f("lb_logits")
    common = [_pc(f("b_ada")[0])] + [_pc(ng[n]) for n in range(6)] + [_pc(cw[j]) for j in range(3)] + \
             [_pc(f("conv_b")[0]), _pc(f("hg_norm_g")[0]), _pc(lbl[0]), _pc(lbl[1])]
    ident = np.eye(128, dtype=np.float32)
    masku = (np.arange(64)[:, None] <= np.arange(64)[None, :]).astype(np.int32)
    in_maps = []
    for b in range(n_cores):
        vec = np.ascontiguousarray(np.concatenate([_pc(c[b])] + common, axis=1)).astype(np.float32)
        assert vec.shape == (128, NV)
        in_maps.append({"x": np.ascontiguousarray(x[b]), "vec": vec, "wada": wada, "wf": wf,
                        "ident": ident, "masku": masku})
    nc = build_program(NT)
    res = run_bass_kernel_spmd(nc, in_maps, core_ids=list(range(n_cores)))
    return np.stack([np.asarray(r["y"], dtype=np.float32) for r in res.results], axis=0)


def kernel(**inputs):
    x = inputs["x"]
    B, S, _ = x.shape
    assert S % TT == 0
    return _run(inputs, S // TT, B)
```

```python
from contextlib import ExitStack
import numpy as np
import concourse.bass as bass
import concourse.mybir as mybir
from concourse.bass_utils import run_bass_kernel_spmd

F32 = mybir.dt.float32
BF16 = mybir.dt.bfloat16
I32 = mybir.dt.int32
ALU = mybir.AluOpType
AF = mybir.ActivationFunctionType

SAME_ENGINE_SYNC = True

D = 1024
DFF = 2816
NJ = DFF // 128
TT = 512
NS = 3
EPS = 1e-6
NV = 184
NW = 66
GH = 2


class Buf:
    __slots__ = ("name", "w", "r", "rd", "dma_sem", "dma_cnt", "excl")

    def __init__(self, name, excl=False):
        self.name = name
        self.excl = excl
        self.w = None
        self.r = {}
        self.rd = []
        self.dma_sem = None
        self.dma_cnt = 0


class Prog:
    ENGS = ("pe", "act", "dve", "pool", "sp")

    def __init__(self, nc, stack):
        self.nc = nc
        self.stack = stack
        self.ops = {e: [] for e in self.ENGS}
        self.seen = {e: {} for e in self.ENGS}
        self.seen_dma = {e: {} for e in self.ENGS}
        self.sems = {e: stack.enter_context(nc.semaphore("s_" + e)) for e in self.ENGS}
        self.nsem = 0
        self.phase = ""

    def _need(self, eng, tok, waits):
        if tok is None:
            return
        if tok[0] == "e":
            _, src, idx = tok
            if src == eng and (eng == "pe" or not SAME_ENGINE_SYNC):
                return
            if self.seen[eng].get(src, -1) >= idx:
                return
            self.seen[eng][src] = idx
            self.ops[src][idx][2] = True
            waits.append(tok)
        else:
            _, buf, cnt = tok
            if self.seen_dma[eng].get(buf.name, 0) >= cnt:
                return
            self.seen_dma[eng][buf.name] = cnt
            waits.append(tok)

    def _deps(self, eng, reads, writes):
        waits = []
        for b in reads:
            self._need(eng, b.w, waits)
        for b in writes:
            self._need(eng, b.w, waits)
            for t in b.r.values():
                self._need(eng, t, waits)
            for t in b.rd:
                self._need(eng, t, waits)
        return waits

    def _commit(self, tok, reads, writes):
        if tok[0] == "e":
            for b in reads:
                b.r[tok[1]] = tok
        else:
            for b in reads:
                b.rd.append(tok)
        for b in writes:
            b.w = tok
            b.r = {}
            b.rd = []

    def op(self, eng, fn, reads=(), writes=()):
        if any(b.excl for b in reads):
            writes = list(writes) + [b for b in reads if b.excl]
            reads = [b for b in reads if not b.excl]
        waits = self._deps(eng, reads, writes)
        idx = len(self.ops[eng])
        self.ops[eng].append([fn, waits, False, None, self.phase])
        tok = ("e", eng, idx)
        self._commit(tok, reads, writes)
        return tok

    def dma(self, eng, fn, reads=(), writes=(), sem_buf=None):
        waits = self._deps(eng, reads, writes)
        if sem_buf.dma_sem is None:
            self.nsem += 1
            sem_buf.dma_sem = self.stack.enter_context(self.nc.semaphore("d_%d" % self.nsem))
        sem_buf.dma_cnt += 1
        tok = ("d", sem_buf, sem_buf.dma_cnt)
        self.ops[eng].append([fn, waits, False, sem_buf, self.phase])
        self._commit(tok, reads, writes)
        return tok

    def emit(self, final_toks):
        nc = self.nc
        final_waits = []
        for t in final_toks:
            self._need("sp", t, final_waits)
        val = {}
        for e in self.ENGS:
            c = 0
            for i, o in enumerate(self.ops[e]):
                if o[2]:
                    c += 1
                    val[(e, i)] = c
        sems = self.sems

        def do_wait(eng, t):
            if t[0] == "e":
                eng.wait_ge(sems[t[1]], val[(t[1], t[2])])
            else:
                eng.wait_ge(t[1].dma_sem, 16 * t[2])

        def run(ename, eng):
            for o in self.ops[ename]:
                for t in o[1]:
                    do_wait(eng, t)
                ins = o[0](eng)
                if o[3] is not None:
                    ins.then_inc(o[3].dma_sem, 16)
                elif o[2]:
                    ins.then_inc(sems[ename], 1)
            if ename == "sp":
                for t in final_waits:
                    do_wait(eng, t)

        with nc.Block() as block:
            @block.tensor
            def _(e):
                run("pe", e)

            @block.scalar
            def _(e):
                run("act", e)

            @block.vector
            def _(e):
                run("dve", e)

            @block.gpsimd
            def _(e):
                run("pool", e)

            @block.sync
            def _(e):
                run("sp", e)


def wtile_sizes():
    sizes = []
    for _ in range(2):
        pass
    order = []
    order += [("ffn1_in", g, 4096) for g in range(11)]
    order += [("ffn1_out", m, 2816) for m in range(8)]
    order += [("vproj", h, 4096) for h in range(2)]
    order += [("head", h, 3072) for h in range(8)]
    order += [("conv", c, 3072) for c in range(8)]
    order += [("merge", m, 4096) for m in range(8)]
    order += [("mixout", h, 4096) for h in range(2)]
    order += [("ffn2_in", g, 4096) for g in range(11)]
    order += [("ffn2_out", m, 2816) for m in range(8)]
    assert len(order) == NW
    return order


WORDER = wtile_sizes()


STOP = 9
LAST_PROG = None


def build_program(NT):
    nc = bass.Bass("TRN2", target_bir_lowering=False)
    SEQ = NT * TT
    x_d = nc.dram_tensor("x", [SEQ, D], F32, kind="ExternalInput").ap()
    vec_d = nc.dram_tensor("vec", [128, NV], F32, kind="ExternalInput").ap()
    wada_d = nc.dram_tensor("wada", [36, 128, 2048], F32, kind="ExternalInput").ap()
    wf_d = nc.dram_tensor("wf", [NW, 128, 4096], F32, kind="ExternalInput").ap()
    ident_d = nc.dram_tensor("ident", [128, 128], F32, kind="ExternalInput").ap()
    mask_d = nc.dram_tensor("masku", [64, 64], I32, kind="ExternalInput").ap()
    y_d = nc.dram_tensor("y", [SEQ, D], F32, kind="ExternalOutput").ap()
    wb_d = nc.dram_tensor("wb", [NW, 128, 4096], BF16, kind="Internal").ap()

    st = ExitStack()
    with st:
        P = Prog(nc, st)

        def sb(name, shape, dt):
            return st.enter_context(nc.sbuf_tensor("sb_" + name, shape, dt))

        ident_f = sb("ident_f", [128, 128], F32)
        ident_b = sb("ident_b", [128, 128], BF16)
        ones_b = sb("ones_b", [128, 128], BF16)
        mask_sb = sb("mask_sb", [64, 64], I32)
        negh = sb("negh", [128, 8], F32)
        epsv = sb("epsv", [128, 8], F32)
        rmask = sb("rmask", [128, TT], F32)
        vec = sb("vec", [128, NV], F32)
        cact = sb("cact", [128, 8], F32)
        ada = sb("ada", [128, 72], F32)
        gp = sb("gp", [128, 24], F32)
        gg = sb("gg", [128, 24], F32)
        lbv = sb("lbv", [128, 40], F32)
        S = sb("S", [128, 8, 128], F32)
        Sbf = sb("Sbf", [128, 8, 128], BF16)
        Sdl = sb("Sdl", [128, 8, 128], F32)
        hist = sb("hist", [128, 8, 2], F32)
        dmid = sb("dmid", [128, 8, 8], F32)
        dl = sb("dl", [128, 8, 8], F32)
        dlm = sb("dlm", [128, 8, 8], F32)
        dtmp = sb("dtmp", [128, 8, 8], F32)
        scs = [sb("scs%d" % i, [64, 64], BF16) for i in range(4)]
        xin_t = [sb("xin%d" % i, [128, D], F32) for i in range(2)]
        xout_t = [sb("xout%d" % i, [128, D], F32) for i in range(2)]
        xT = sb("xT", [128, 8, TT], F32)
        xT1 = sb("xT1", [128, 8, TT], F32)
        rstd2 = sb("rstd2", [128, TT], F32)
        hT = sb("hT", [128, 8, TT], BF16)
        yT = sb("yT", [128, 8, TT], F32)
        sq = [sb("sq%d" % i, [128, TT], BF16) for i in range(3)]
        TR = [sb("tr%d" % i, [128, TT], F32) for i in range(3)]
        ms = sb("ms", [128, TT], F32)
        rstd = sb("rstd", [128, TT], F32)
        AR = sb("AR", [128, 24, TT], BF16)
        vT = sb("vT", [128, 8, TT], BF16)
        gsil = sb("gsil", [128, 8, TT], BF16)
        ogT = sb("ogT", [128, 8, TT], BF16)
        mT = sb("mT", [128, 8, TT], BF16)
        ccvw = [sb("ccvw%d" % i, [128, TT + 2], F32) for i in range(2)]
        ktok = sb("ktok", [64, GH, 8, 128], BF16)
        vtok = sb("vtok", [64, GH, 8, 128], BF16)
        wslot = [sb("wslot%d" % i, [128, 4096], BF16) for i in range(NS)]
        wstg = [sb("wstg%d" % i, [128, 1024], F32) for i in range(3)]

        banks = [st.enter_context(nc.psum_tensor("bank%d" % i, [128, 512], F32)) for i in range(8)]

        def bl(prefix, n, excl=False):
            return [Buf("%s%d" % (prefix, i), excl) for i in range(n)]

        bCONST = Buf("const")
        bVEC = Buf("vec")
        bXT, bHT, bYT = bl("xT", 8), bl("hT", 8), bl("yT", 8)
        bAR = bl("AR", 24)
        bVT, bGS, bOG, bMT = bl("vT", 8), bl("gs", 8), bl("og", 8), bl("mT", 8)
        bSQ, bTR = bl("sq", 3), bl("tr", 3)
        bMS, bRSTD = Buf("ms"), Buf("rstd")
        bWARM = Buf("warm")
        bRSTD2 = Buf("rstd2")
        bXT1 = bl("xU", 8)
        xTb, bXTb = [xT, xT1], [bXT, bXT1]
        cur = {"X": xT, "bX": bXT}
        bBANK = bl("bank", 8, True)
        bSS = bBANK[4]
        RING = [0, 1, 2, 3, 5, 6, 7]
        bSCS = bl("scs", 4)
        bS, bSbf, bSdl = bl("S", 8), bl("Sbf", 8), bl("Sdl", 8)
        bHIST = bl("hist", 8)
        bDEC = bl("dec", 8)
        bDTMP = Buf("dtmp")
        bKTOK, bVTOK = bl("ktok", GH), bl("vtok", GH)
        bXIN, bXOUT = bl("xin", 2), bl("xout", 2)
        bWS = bl("ws", NS)
        bWSTG = bl("wstg", 3)
        bCCVW = bl("ccvw", 2)
        bWB = [Buf("wb%d" % w) for w in range(NW)]
        bADA = bSS

        rr = {"wstg": 0, "bank": 0, "tr": 0, "sq": 0, "sc": 0, "kv": 0, "scs": 0, "ccvw": 0, "xin": 0, "xout": 0}

        def nxt(kind, n):
            i = rr[kind]
            rr[kind] = (i + 1) % n
            return i

        def bank():
            i = RING[nxt("bank", 7)]
            return banks[i], bBANK[i]

        def tr():
            i = nxt("tr", 3)
            return TR[i], bTR[i]

        def act_(out, in_, func, R, W, **kw):
            return P.op("act", lambda e: e.activation(out=out, in_=in_, func=func, **kw), reads=R, writes=W)

        def tt_(eng, out, a, b, op, R, W):
            return P.op(eng, lambda e: e.tensor_tensor(out=out, in0=a, in1=b, op=op), reads=R, writes=W)

        def ts_(eng, out, a, s1, s2, op0, op1, R, W):
            return P.op(eng, lambda e: e.tensor_scalar(out=out, in0=a, scalar1=s1, scalar2=s2, op0=op0, op1=op1),
                        reads=R, writes=W)

        def stt_(out, a, s, b, op0, op1, R, W):
            return P.op("dve", lambda e: e.scalar_tensor_tensor(out=out, in0=a, scalar=s, in1=b, op0=op0, op1=op1),
                        reads=R, writes=W)

        def copy_(eng, out, in_, R, W):
            if eng == "act":
                return act_(out, in_, AF.Copy, R, W)
            return P.op(eng, lambda e: e.tensor_copy(out=out, in_=in_), reads=R, writes=W)

        def mm_(out, lhsT, rhs, start, stop, R, W):
            return P.op("pe", lambda e: e.matmul(out, lhsT=lhsT, rhs=rhs, start=start, stop=stop), reads=R, writes=W)

        def tp_(out, in_, ident, R, W):
            return P.op("pe", lambda e: e.transpose(out=out, in_=in_, identity=ident), reads=R, writes=W)

        def memset_(eng, ap, v, W):
            return P.op(eng, lambda e: e.memset(ap, v), writes=W)

        P.dma("sp", lambda e: e.dma_start(out=vec[:], in_=vec_d[:, :]), writes=[bVEC], sem_buf=bVEC)
        bID = Buf("identf")
        bMK = Buf("mask")
        P.dma("sp", lambda e: e.dma_start(out=ident_f[:], in_=ident_d[:, :]), writes=[bID], sem_buf=bID)
        P.dma("sp", lambda e: e.dma_start(out=mask_sb[:], in_=mask_d[:, :]), writes=[bMK], sem_buf=bMK)
        copy_("dve", ident_b[:], ident_f[:], [bID], [bCONST])
        memset_("pool", ones_b[:], 1.0, [bCONST])
        memset_("pool", negh[:], -0.5, [bCONST])
        memset_("pool", epsv[:], EPS, [bCONST])
        memset_("pool", rmask[:], 1.0, [bCONST])
        memset_("pool", rmask[:].rearrange("p (c t) -> p c t", t=64)[:, :, 0:1], 0.0, [bCONST])
        memset_("pool", S[:], 0.0, bS)
        memset_("pool", hist[:], 0.0, bHIST)
        for i in range(4):
            memset_("pool", scs[i][:], 0.0, [bSCS[i]])

        V_C, V_BADA, V_NG, V_CW, V_CB, V_HGN, V_LB = 0, 8, 80, 128, 152, 160, 168
        act_(cact[:], vec[:, V_C:V_C + 8], AF.Silu, [bVEC], [bCONST])
        for i in range(36):
            k = i % 2
            stg2 = yT[:, 4 * k:4 * k + 4, :].rearrange("p a b -> p (a b)")
            stg = stg2.rearrange("p (c n) -> p c n", c=8)
            sbufs = bYT[4 * k:4 * k + 4]
            P.dma("sp", lambda e, stg2=stg2, i=i: e.dma_start(out=stg2, in_=wada_d[i, :, :]),
                  writes=sbufs, sem_buf=sbufs[0])
            for nb in range(2):
                blk = i * 2 + nb
                for c in range(8):
                    mm_(banks[4][:, blk:blk + 1], stg[:, c, nb * 128:(nb + 1) * 128], cact[:, c:c + 1],
                        c == 0, c == 7, sbufs + [bCONST], [bADA])
        tt_("dve", ada[:], banks[4][:, 0:72], vec[:, V_BADA:V_BADA + 72], ALU.add, [bADA, bVEC], [bCONST])
        RW = [0.5, 1.0, 0.5]
        for i in range(3):
            sh_c, sc_c, g_c = (3 * i) * 8, (3 * i + 1) * 8, (3 * i + 2) * 8
            stt_(gp[:, i * 8:(i + 1) * 8], ada[:, sc_c:sc_c + 8], 1.0,
                 vec[:, V_NG + (2 * i) * 8:V_NG + (2 * i) * 8 + 8], ALU.add, ALU.mult, [bCONST, bVEC], [bCONST])
            stt_(gg[:, i * 8:(i + 1) * 8], ada[:, g_c:g_c + 8], RW[i],
                 vec[:, V_NG + (2 * i + 1) * 8:V_NG + (2 * i + 1) * 8 + 8], ALU.mult, ALU.mult, [bCONST, bVEC], [bCONST])

        def shv(i, c):
            return ada[:, (3 * i) * 8 + c:(3 * i) * 8 + c + 1]

        tt_("dve", lbv[:, 32:40], vec[:, V_LB:V_LB + 8], vec[:, V_LB + 8:V_LB + 16], ALU.subtract, [bVEC], [bCONST])
        act_(lbv[:, 0:8], lbv[:, 32:40], AF.Sigmoid, [bCONST], [bCONST])
        ts_("dve", lbv[:, 8:16], lbv[:, 0:8], 0.5, 0.5, ALU.mult, ALU.add, [bCONST], [bCONST])
        ts_("dve", lbv[:, 16:24], lbv[:, 0:8], -0.5, 0.5, ALU.mult, ALU.add, [bCONST], [bCONST])
        ts_("dve", lbv[:, 24:32], lbv[:, 0:8], 0.5, -0.5, ALU.mult, ALU.add, [bCONST], [bCONST])

        def _en(kind):
            if kind.startswith("ffn1"):
                return STOP >= 4
            if kind.startswith("ffn2"):
                return STOP >= 6
            return STOP >= 5
        wseq = [w for _ in range(NT) for w in range(NW) if _en(WORDER[w][0])]
        ws = {"issued": 0, "used": 0}

        n_first = len(wseq) // NT

        def w_issue():
            n = ws["issued"]
            w = wseq[n]
            E = WORDER[w][2]
            s = n % NS
            if n < n_first:
                for off in range(0, E, 1024):
                    m = min(1024, E - off)
                    k = nxt("wstg", 3)
                    P.dma("sp", lambda e, k=k, off=off, m=m: e.dma_start(out=wstg[k][:, 0:m], in_=wf_d[w, :, off:off + m]),
                          writes=[bWSTG[k]], sem_buf=bWSTG[k])
                    ceng = ("pool", "dve", "pool", "act")[(off // 1024) % 4]
                    copy_(ceng, wslot[s][:, off:off + m], wstg[k][:, 0:m], [bWSTG[k]], [bWS[s]])
                P.dma("pool", lambda e: e.dma_start(out=wb_d[w, :, 0:E], in_=wslot[s][:, 0:E]),
                      reads=[bWS[s]], writes=[bWB[w]], sem_buf=bWB[w])
            else:
                P.dma("sp", lambda e: e.dma_start(out=wslot[s][:, 0:E], in_=wb_d[w, :, 0:E]),
                      reads=[bWB[w]], writes=[bWS[s]], sem_buf=bWS[s])
            ws["issued"] = n + 1

        def w_next(kind):
            n = ws["used"]
            assert WORDER[wseq[n]][0] == kind, (WORDER[wseq[n]], kind)
            while ws["issued"] < min(len(wseq), n + NS):
                w_issue()
            ws["used"] = n + 1
            s = n % NS
            return wslot[s], bWS[s]

        def rms_stats_finish(inv_n, ss_ap=None, bss=None, rs=None, brs=None):
            if ss_ap is None:
                ss_ap, bss, rs, brs = banks[4][:, :], bSS, rstd, bRSTD
                m_, bm_ = ms, bMS
            else:
                m_, bm_ = tr()
            act_(m_[:], ss_ap, AF.Ln, [bss, bCONST], [bm_], scale=inv_n, bias=epsv[:, 0:1])
            act_(rs[:], m_[:], AF.Exp, [bm_], [brs], scale=-0.5)

        def prenorm(i, alt=False):
            P.phase = "prenorm%d" % i
            X, bX = cur["X"], cur["bX"]
            if alt:
                ss_t, bss = bank()
                ss_ap, rs, brs = ss_t[:, :], rstd2, bRSTD2
            else:
                ss_ap, bss, rs, brs = banks[4][:, :], bSS, rstd, bRSTD
            for c in range(8):
                r = nxt("sq", 3)
                act_(sq[r][:], X[:, c, :], AF.Square, [bX[c]], [bSQ[r]])
                mm_(ss_ap, ones_b[:], sq[r][:], c == 0, c == 7, [bCONST, bSQ[r]], [bss])
            rms_stats_finish(1.0 / D, ss_ap, bss, rs, brs)
            for c in range(8):
                t, bt_ = tr()
                tt_("dve", t[:], X[:, c, :], rs[:], ALU.mult, [bX[c], brs], [bt_])
                act_(hT[:, c, :], t[:], AF.Identity, [bt_, bCONST], [bHT[c]],
                     scale=gp[:, i * 8 + c:i * 8 + c + 1], bias=shv(i, c))

        def y_block_done(m, y_ps, by):
            copy_("dve", yT[:, m, :], y_ps[:, :], [by], [bYT[m]])
            r = nxt("sq", 3)
            act_(sq[r][:], yT[:, m, :], AF.Square, [bYT[m]], [bSQ[r]])
            return r

        def postnorm(i):
            P.phase = "postnorm%d" % i
            X, bX = cur["X"], cur["bX"]
            rms_stats_finish(1.0 / D)
            prev = None
            for c in range(9):
                cur_t = None
                if c < 8:
                    t, bt_ = tr()
                    tt_("dve", t[:], yT[:, c, :], rstd[:], ALU.mult, [bYT[c], bRSTD], [bt_])
                    cur_t = (c, t, bt_)
                if prev is not None:
                    pc, pt, pbt = prev
                    stt_(X[:, pc, :], pt[:], gg[:, i * 8 + pc:i * 8 + pc + 1], X[:, pc, :], ALU.mult, ALU.add,
                         [pbt, bCONST, bX[pc]], [bX[pc]])
                prev = cur_t

        def out_blocks(kind, nblk_per_tile, ntiles, rhs_of, rbufs_of, nk, hook=None):
            act_(negh[:, 0:1], epsv[:, 0:1], AF.Ln, [bCONST], [bWARM])
            pend = None
            for wt_i in range(ntiles):
                wt, bw = w_next(kind)
                wv = wt[:, 0:nk * nblk_per_tile * 128].rearrange("p (k n) -> p k n", k=nk)
                for mm in range(nblk_per_tile):
                    m = wt_i * nblk_per_tile + mm
                    y_ps, by = bank()
                    for k in range(nk):
                        mm_(y_ps[:, :], wv[:, k, mm * 128:(mm + 1) * 128], rhs_of(k), k == 0, k == nk - 1,
                            [bw] + rbufs_of(k), [by])
                    if pend is not None:
                        mm_(banks[4][:, :], ones_b[:], sq[pend[0]][:], pend[1] == 0, False,
                            [bCONST, bSQ[pend[0]]], [bSS])
                    r = y_block_done(m, y_ps, by)
                    pend = (r, m)
                    if hook is not None and m == 1:
                        mm_(banks[4][:, :], ones_b[:], sq[pend[0]][:], pend[1] == 0, False,
                            [bCONST, bSQ[pend[0]]], [bSS])
                        pend = None
                        hook()
            mm_(banks[4][:, :], ones_b[:], sq[pend[0]][:], False, True, [bCONST, bSQ[pend[0]]], [bSS])

        def ffn(i, kin, kout, pre=True, mid_hook=None, out_hook=None):
            if pre:
                prenorm(i)
            for g in range(11):
                P.phase = "ffn%d_in" % i
                if mid_hook is not None and g == 2:
                    mid_hook()
                    P.phase = "ffn%d_in" % i
                wt, bw = w_next(kin)
                wv = wt[:, :].rearrange("p (c n) -> p c n", c=8)
                for jj in range(2):
                    j = 2 * g + jj
                    a_ps, ba_ = bank()
                    b_ps, bb_ = bank()
                    for c in range(8):
                        mm_(a_ps[:, :], wv[:, c, jj * 128:(jj + 1) * 128], hT[:, c, :], c == 0, c == 7,
                            [bw, bHT[c]], [ba_])
                    for c in range(8):
                        mm_(b_ps[:, :], wv[:, c, 256 + jj * 128:256 + (jj + 1) * 128], hT[:, c, :], c == 0, c == 7,
                            [bw, bHT[c]], [bb_])
                    t, bt_ = tr()
                    act_(t[:], a_ps[:, :], AF.Silu, [ba_], [bt_])
                    tt_("dve", AR[:, j, :], t[:], b_ps[:, :], ALU.mult, [bt_, bb_], [bAR[j]])
            P.phase = "ffn%d_out" % i
            def oh():
                out_hook()
                P.phase = "ffn%d_out" % i
            out_blocks(kout, 1, 8, lambda k: AR[:, k, :], lambda k: [bAR[k]], NJ, hook=(oh if out_hook else None))
            postnorm(i)

        def mixer():
            i = 1
            prenorm(i)
            P.phase = "vproj"
            for hg_ in range(2):
                wt, bw = w_next("vproj")
                wv = wt[:, :].rearrange("p (c n) -> p c n", c=8)
                for hh in range(4):
                    hd = hg_ * 4 + hh
                    v_ps, bv_ = bank()
                    for k in range(8):
                        mm_(v_ps[:, :], wv[:, k, hh * 128:(hh + 1) * 128], hT[:, k, :], k == 0, k == 7,
                            [bw, bHT[k]], [bv_])
                    copy_("act" if hh % 2 == 0 else "dve", vT[:, hd, :], v_ps[:, :], [bv_], [bVT[hd]])
            qT = lambda hd: AR[:, 8 + hd, :]
            kT = lambda hd: AR[:, 16 + hd, :]
            P.phase = "heads"

            def TY(i):
                return yT[:, i, :], bYT[i]

            for pr in range(4):
                hds = [2 * pr, 2 * pr + 1]
                proj = {}
                for hi_, hd in enumerate(hds):
                    wt, bw = w_next("head")
                    wv = wt[:, 0:3072].rearrange("p (c n) -> p c n", c=8)
                    q_ps, bqp = bank()
                    f_ps, bfp = bank()
                    g_ps, bgp = bank()
                    for (ps_, bp_, o_) in ((q_ps, bqp, 0), (f_ps, bfp, 128), (g_ps, bgp, 256)):
                        for k in range(8):
                            mm_(ps_[:, :], wv[:, k, o_:o_ + 128], hT[:, k, :], k == 0, k == 7, [bw, bHT[k]], [bp_])
                    proj[hd] = (q_ps, bqp, f_ps, bfp, g_ps, bgp)
                for hi_, hd in enumerate(hds):
                    (q_ps, bqp, f_ps, bfp, g_ps, bgp) = proj[hd]
                    T1, b1 = TY(4 * hi_)
                    act_(qT(hd), q_ps[:, :], AF.Silu, [bqp], [bAR[8 + hd]])
                    act_(T1, f_ps[:, :], AF.Tanh, [bfp], [b1], scale=0.5)
                    act_(gsil[:, hd, :], g_ps[:, :], AF.Silu, [bgp], [bGS[hd]])
                for hi_, hd in enumerate(hds):
                    T1, b1 = TY(4 * hi_)
                    T2, b2 = TY(4 * hi_ + 1)
                    c0 = lbv[:, 8 + hd:9 + hd]
                    c1 = lbv[:, 16 + hd:17 + hd]
                    nc1 = lbv[:, 24 + hd:25 + hd]
                    ts_("pool", T2, T1, nc1, c1, ALU.mult, ALU.add, [b1, bCONST], [b2])
                    ts_("dve", T1, T1, c1, c0, ALU.mult, ALU.add, [b1, bCONST], [b1])
                for hi_, hd in enumerate(hds):
                    T1, b1 = TY(4 * hi_)
                    act_(T1, T1, AF.Ln, [b1], [b1])
                for hi_, hd in enumerate(hds):
                    T1, b1 = TY(4 * hi_)
                    T3, b3 = TY(4 * hi_ + 2)
                    P.op("dve", lambda e, T1=T1, T3=T3: e.tensor_tensor_scan(out=T3, data0=rmask[:], data1=T1, initial=0.0,
                                                                            op0=ALU.mult, op1=ALU.add),
                         reads=[bCONST, b1], writes=[b3])
                    a3 = T3.rearrange("p (c t) -> p c t", t=64)
                    tt_("dve", dtmp[:, hd, :], a3[:, :, 63], a3[:, :, 31], ALU.subtract, [b3], [bDTMP])
                for hi_, hd in enumerate(hds):
                    T3, b3 = TY(4 * hi_ + 2)
                    a3 = T3.rearrange("p (c t) -> p c t", t=64)
                    act_(dmid[:, hd, :], a3[:, :, 31], AF.Exp, [b3], [bDEC[hd]])
                    act_(dl[:, hd, :], a3[:, :, 63], AF.Exp, [b3], [bDEC[hd]])
                    act_(dlm[:, hd, :], dtmp[:, hd, :], AF.Exp, [bDTMP], [bDEC[hd]])
                for hi_, hd in enumerate(hds):
                    T1, b1 = TY(4 * hi_)
                    T2, b2 = TY(4 * hi_ + 1)
                    T3, b3 = TY(4 * hi_ + 2)
                    T4, b4 = TY(4 * hi_ + 3)
                    a3 = T3.rearrange("p (c t) -> p c t", t=64)
                    E3 = T4.rearrange("p (c t) -> p c t", t=64)
                    tt_("dve", E3, a3[:, :, 31:32].to_broadcast([128, 8, 64]), a3, ALU.subtract, [b3], [b4])
                    act_(T1, T4, AF.Exp, [b4], [b1])
                    act_(T4, T4, AF.Exp, [b4], [b4], scale=-1.0)
                    tt_("dve", kT(hd), T2, T1, ALU.mult, [b2, b1], [bAR[16 + hd]])
                    tt_("pool", qT(hd), qT(hd), T4, ALU.mult, [bAR[8 + hd], b4], [bAR[8 + hd]])

            P.phase = "conv"
            for c in range(8):
                wt, bw = w_next("conv")
                wv = wt[:, 0:3072].rearrange("p (c n) -> p c n", c=8)
                cc_ps, bcc = bank()
                cv_ps, bcv = bank()
                cb_ps, bcb = bank()
                for (ps_, bp_, o_) in ((cc_ps, bcc, 0), (cv_ps, bcv, 128), (cb_ps, bcb, 256)):
                    for k in range(8):
                        mm_(ps_[:, :], wv[:, k, o_:o_ + 128], hT[:, k, :], k == 0, k == 7, [bw, bHT[k]], [bp_])
                t, bt_ = tr()
                copy_("act", t[:], cv_ps[:, :], [bcv], [bt_])
                r = nxt("ccvw", 2)
                cw = ccvw[r]
                bcw = bCCVW[r]
                copy_("pool", cw[:, 0:2], hist[:, c, :], [bHIST[c]], [bcw])
                tt_("dve", cw[:, 2:TT + 2], cc_ps[:, :], t[:], ALU.mult, [bcc, bt_, bcw], [bcw])
                copy_("pool", hist[:, c, :], cw[:, TT:TT + 2], [bcw], [bHIST[c]])
                a_, ba2 = tr()
                w0 = vec[:, V_CW + 0 * 8 + c:V_CW + 0 * 8 + c + 1]
                w1 = vec[:, V_CW + 1 * 8 + c:V_CW + 1 * 8 + c + 1]
                w2 = vec[:, V_CW + 2 * 8 + c:V_CW + 2 * 8 + c + 1]
                cbv = vec[:, V_CB + c:V_CB + c + 1]
                ts_("dve", a_[:], cw[:, 2:TT + 2], w2, cbv, ALU.mult, ALU.add, [bcw, bVEC], [ba2])
                stt_(a_[:], cw[:, 1:TT + 1], w1, a_[:], ALU.mult, ALU.add, [bcw, bVEC, ba2], [ba2])
                stt_(a_[:], cw[:, 0:TT], w0, a_[:], ALU.mult, ALU.add, [bcw, bVEC, ba2], [ba2])
                tt_("dve", AR[:, c, :], a_[:], cb_ps[:, :], ALU.mult, [ba2, bcb], [bAR[c]])
            for gi in range(8 // GH):
                heads = list(range(gi * GH, (gi + 1) * GH))
                P.phase = "rec_tp"
                for hi_, hd in enumerate(heads):
                    for (src, bsrc, dst, bdst) in ((kT(hd), bAR[16 + hd], ktok, bKTOK[hi_]),
                                                   (vT[:, hd, :], bVT[hd], vtok, bVTOK[hi_])):
                        tp_ps, btp = bank()
                        psb = tp_ps[:, :].bitcast(BF16)
                        for ch in range(8):
                            tp_(psb[0:64, ch * 128:(ch + 1) * 128], src[:, ch * 64:(ch + 1) * 64], ident_b[:],
                                [bsrc, bCONST], [btp])
                        copy_("dve" if hi_ % 2 == 0 else "act",
                              dst[:, hi_, :, :].rearrange("p a b -> p (a b)"), psb[0:64, 0:1024], [btp], [bdst])
                obanks = [bank() for _ in heads]
                scb = [bank(), bank()]
                kvb = bank()
                P.phase = "rec"
                steps = [(ch, hi_, hd) for ch in range(8) for hi_, hd in enumerate(heads)]

                def issue_sc(n):
                    ch, hi_, hd = steps[n]
                    cs = slice(ch * 64, (ch + 1) * 64)
                    sc_t, bsc = scb[n % 2]
                    sc_ps = sc_t[0:64, 0:64]
                    mm_(sc_ps, kT(hd)[:, cs], qT(hd)[:, cs], True, True, [bAR[16 + hd], bAR[8 + hd]], [bsc])
                    ri = nxt("scs", 4)
                    P.op("dve", lambda e, ri=ri, sc_ps=sc_ps: e.copy_predicated(out=scs[ri][:], mask=mask_sb[:], data=sc_ps),
                         reads=[bsc, bMK], writes=[bSCS[ri]])
                    return ri

                ri_next = issue_sc(0)
                for n, (ch, hi_, hd) in enumerate(steps):
                    o_ps, bo = obanks[hi_]
                    cs = slice(ch * 64, (ch + 1) * 64)
                    ri = ri_next
                    act_(Sbf[:, hd, :], S[:, hd, :], AF.Identity, [bS[hd], bDEC[hd]], [bSbf[hd]],
                         scale=dmid[:, hd, ch:ch + 1])
                    ts_("dve", Sdl[:, hd, :], S[:, hd, :], dl[:, hd, ch:ch + 1], None, ALU.mult, ALU.bypass,
                        [bS[hd], bDEC[hd]], [bSdl[hd]])
                    if n + 1 < len(steps):
                        ri_next = issue_sc(n + 1)
                    mm_(o_ps[:, cs], vtok[:, hi_, ch, :], scs[ri][:], True, False, [bVTOK[hi_], bSCS[ri]], [bo])
                    mm_(o_ps[:, cs], Sbf[:, hd, :], qT(hd)[:, cs], False, True, [bSbf[hd], bAR[8 + hd]], [bo])
                    kv_ps = kvb[0][:, 0:128]
                    mm_(kv_ps, ktok[:, hi_, ch, :], vtok[:, hi_, ch, :], True, True, [bKTOK[hi_], bVTOK[hi_]], [kvb[1]])
                    stt_(S[:, hd, :], kv_ps, dlm[:, hd, ch:ch + 1], Sdl[:, hd, :], ALU.mult, ALU.add,
                         [kvb[1], bDEC[hd], bSdl[hd]], [bS[hd]])
                P.phase = "onorm"
                for hi_, hd in enumerate(heads):
                    o_ps, bo = obanks[hi_]
                    r = nxt("sq", 3)
                    act_(sq[r][:], o_ps[:, :], AF.Square, [bo], [bSQ[r]])
                    mm_(banks[4][:, :], ones_b[:], sq[r][:], True, True, [bCONST, bSQ[r]], [bSS])
                    rms_stats_finish(1.0 / 128)
                    t, bt_ = tr()
                    tt_("dve", t[:], o_ps[:, :], rstd[:], ALU.mult, [bo, bRSTD], [bt_])
                    stt_(ogT[:, hd, :], t[:], vec[:, V_HGN + hd:V_HGN + hd + 1], gsil[:, hd, :], ALU.mult, ALU.mult,
                         [bt_, bVEC, bGS[hd]], [bOG[hd]])
            P.phase = "merge"
            for m in range(8):
                wt, bw = w_next("merge")
                wv = wt[:, :].rearrange("p (c n) -> p c n", c=8)
                ya_ps, bya = bank()
                yb_ps, byb = bank()
                ga_ps, bga = bank()
                gb_ps, bgb = bank()
                for k in range(8):
                    mm_(ya_ps[:, :], wv[:, k, 0:128], AR[:, k, :], k == 0, k == 7, [bw, bAR[k]], [bya])
                for k in range(8):
                    mm_(yb_ps[:, :], wv[:, k, 128:256], ogT[:, k, :], k == 0, k == 7, [bw, bOG[k]], [byb])
                for k in range(8):
                    mm_(ga_ps[:, :], wv[:, k, 256:384], hT[:, k, :], k == 0, k == 7, [bw, bHT[k]], [bga])
                for k in range(8):
                    mm_(gb_ps[:, :], wv[:, k, 384:512], hT[:, k, :], k == 0, k == 7, [bw, bHT[k]], [bgb])
                o4 = 4 * (m % 2)
                Y0, y0 = yT[:, o4, :], bYT[o4]
                Y1, y1 = yT[:, o4 + 1, :], bYT[o4 + 1]
                Y2, y2 = yT[:, o4 + 2, :], bYT[o4 + 2]
                Y3, y3 = yT[:, o4 + 3, :], bYT[o4 + 3]
                act_(Y0, ga_ps[:, :], AF.Sigmoid, [bga], [y0])
                act_(Y1, gb_ps[:, :], AF.Sigmoid, [bgb], [y1])
                tt_("dve", Y2, Y0, ya_ps[:, :], ALU.mult, [y0, bya], [y2])
                tt_("dve", Y3, Y1, yb_ps[:, :], ALU.mult, [y1, byb], [y3])
                tt_("pool", mT[:, m, :], Y2, Y3, ALU.add, [y2, y3], [bMT[m]])
            P.phase = "mixout"
            out_blocks("mixout", 4, 2, lambda k: mT[:, k, :], lambda k: [bMT[k]], 8)
            postnorm(i)

        def load_x(it, s):
            r = nxt("xin", 2)
            row = it * TT + s * 128
            P.dma("sp", lambda e: e.dma_start(out=xin_t[r][:], in_=x_d[row:row + 128, :]),
                  writes=[bXIN[r]], sem_buf=bXIN[r])
            return r

        final = []
        xq = []

        def xin(it):
            P.phase = "xin"
            X, bX = cur["X"], cur["bX"]
            if it == 0:
                xq.append(load_x(0, 0))
                xq.append(load_x(0, 1))
            for s in range(4):
                r = xq.pop(0)
                for half in range(2):
                    tp_ps, btp = bank()
                    for cc in range(4):
                        c = half * 4 + cc
                        tp_(tp_ps[:, cc * 128:(cc + 1) * 128], xin_t[r][:, c * 128:(c + 1) * 128], ident_f[:],
                            [bXIN[r], bID], [btp])
                    copy_("act" if half == 0 else "dve",
                          X[:, half * 4:half * 4 + 4, s * 128:(s + 1) * 128],
                          tp_ps[:, :].rearrange("p (c t) -> p c t", c=4), [btp], bX[half * 4:half * 4 + 4])
                if s < 2:
                    xq.append(load_x(it, s + 2))
                elif it + 1 < NT:
                    xq.append(load_x(it + 1, s - 2))

        def xout(it, X, bX):
            P.phase = "xout"
            for s in range(4):
                r = nxt("xout", 2)
                for half in range(2):
                    tp_ps, btp = bank()
                    for cc in range(4):
                        c = half * 4 + cc
                        tp_(tp_ps[:, cc * 128:(cc + 1) * 128], X[:, c, s * 128:(s + 1) * 128], ident_f[:],
                            [bX[c], bID], [btp])
                    copy_("act" if half == 0 else "dve", xout_t[r][:, half * 512:(half + 1) * 512], tp_ps[:, :],
                          [btp], [bXOUT[r]])
                row = it * TT + s * 128
                final.append(P.dma("act", lambda e, r=r, row=row: e.dma_start(out=y_d[row:row + 128, :], in_=xout_t[r][:]),
                                   reads=[bXOUT[r]], sem_buf=bXOUT[r]))

        def prep(it, alt):
            cur["X"], cur["bX"] = xTb[it % 2], bXTb[it % 2]
            xin(it)
            prenorm(0, alt=alt)

        prep(0, False)
        pend_xout = None
        for it in range(NT):
            cur["X"], cur["bX"] = xTb[it % 2], bXTb[it % 2]

            def out_hook(it=it):
                if it + 1 < NT:
                    prep(it + 1, True)
                    cur["X"], cur["bX"] = xTb[it % 2], bXTb[it % 2]

            hooked = False
            if STOP >= 4:
                ffn(0, "ffn1_in", "ffn1_out", pre=False, mid_hook=pend_xout)
                pend_xout = None
            if pend_xout is not None:
                pend_xout()
                pend_xout = None
            if STOP >= 5:
                mixer()
            if STOP >= 6:
                ffn(2, "ffn2_in", "ffn2_out", out_hook=out_hook)
            else:
                out_hook()
            pend_xout = (lambda it=it, X=cur["X"], bX=cur["bX"]: xout(it, X, bX))
        pend_xout()
        P.emit(final)
        global LAST_PROG
        LAST_PROG = P
    return nc


def _pc(v):
    return np.ascontiguousarray(v.reshape(-1, 128).T)


def _in_tile(W, cols):
    sub = W[:, cols]
    n = sub.shape[1]
    return np.ascontiguousarray(sub.reshape(8, 128, n).transpose(1, 0, 2)).reshape(128, 8 * n)


def _layout_weights(w_ffn1_in, w_ffn1_out, w_mix_in, w_conv_out, w_hg_out, w_mix_out, w_ffn2_in, w_ffn2_out):
    wf = np.zeros((NW, 128, 4096), np.float32)
    r128 = np.arange(128)
    for w, (kind, idx, E) in enumerate(WORDER):
        if kind in ("ffn1_in", "ffn2_in"):
            W = w_ffn1_in if kind == "ffn1_in" else w_ffn2_in
            g = idx
            cols = np.concatenate([(2 * g) * 128 + r128, (2 * g + 1) * 128 + r128,
                                   DFF + (2 * g) * 128 + r128, DFF + (2 * g + 1) * 128 + r128])
            t = _in_tile(W, cols)
        elif kind in ("ffn1_out", "ffn2_out"):
            W = w_ffn1_out if kind == "ffn1_out" else w_ffn2_out
            m = idx
            sub = W[:, m * 128:(m + 1) * 128]
            t = np.ascontiguousarray(sub.reshape(NJ, 128, 128).transpose(1, 0, 2)).reshape(128, NJ * 128)
        elif kind == "conv":
            c = idx
            cols = np.concatenate([1024 + c * 128 + r128, 2048 + c * 128 + r128, 0 + c * 128 + r128])
            t = _in_tile(w_mix_in, cols)
        elif kind == "vproj":
            cols = 5 * 1024 + idx * 512 + np.arange(512)
            t = _in_tile(w_mix_in, cols)
        elif kind == "head":
            hd = idx
            cols = np.concatenate([3 * 1024 + hd * 128 + r128, 4 * 1024 + hd * 128 + r128, 6 * 1024 + hd * 128 + r128])
            t = _in_tile(w_mix_in, cols)
        elif kind == "merge":
            m = idx
            mc = m * 128 + r128
            t = np.concatenate([
                _in_tile(w_conv_out, mc).reshape(128, 8, 128),
                _in_tile(w_hg_out, mc).reshape(128, 8, 128),
                _in_tile(w_mix_in, 7 * 1024 + mc).reshape(128, 8, 128),
                _in_tile(w_mix_in, 8 * 1024 + mc).reshape(128, 8, 128)], axis=2).reshape(128, 4096)
        elif kind == "mixout":
            cols = idx * 512 + np.arange(512)
            t = _in_tile(w_mix_out, cols)
        assert t.shape[1] == E, (kind, t.shape, E)
        wf[w, :, :E] = t
    return wf


def _run(inputs, NT, n_cores):
    f = lambda k: np.asarray(inputs[k], dtype=np.float32)
    x = f("x")
    c = f("c")
    w_ada = f("w_ada")[0]
    wada = np.ascontiguousarray(
        w_ada.reshape(8, 128, 36, 256).transpose(2, 1, 0, 3)).reshape(36, 128, 2048)
    wf = _layout_weights(f("w_ffn1_in")[0], f("w_ffn1_out")[0], f("w_mix_in")[0], f("w_conv_out")[0],
                         f("w_hg_out")[0], f("w_mix_out")[0], f("w_ffn2_in")[0], f("w_ffn2_out")[0])
    ng = f("norm_gains")[0]
    cw = f("conv_w")[0]
    lbl = f("lb_logits")
    common = [_pc(f("b_ada")[0])] + [_pc(ng[n]) for n in range(6)] + [_pc(cw[j]) for j in range(3)] + \
             [_pc(f("conv_b")[0]), _pc(f("hg_norm_g")[0]), _pc(lbl[0]), _pc(lbl[1])]
    ident = np.eye(128, dtype=np.float32)
    masku = (np.arange(64)[:, None] <= np.arange(64)[None, :]).astype(np.int32)
    in_maps = []
    for b in range(n_cores):
        vec = np.ascontiguousarray(np.concatenate([_pc(c[b])] + common, axis=1)).astype(np.float32)
        assert vec.shape == (128, NV)
        in_maps.append({"x": np.ascontiguousarray(x[b]), "vec": vec, "wada": wada, "wf": wf,
                        "ident": ident, "masku": masku})
    nc = build_program(NT)
    res = run_bass_kernel_spmd(nc, in_maps, core_ids=list(range(n_cores)))
    return np.stack([np.asarray(r["y"], dtype=np.float32) for r in res.results], axis=0)


def kernel(**inputs):
    x = inputs["x"]
    B, S, _ = x.shape
    assert S % TT == 0
    return _run(inputs, S // TT, B)
```
